# Optimizing a Trainium2 kernel written in Bass

```python
import jax, jax.numpy as jnp
from jax import lax
import numpy as np

D_MODEL = 4096
BATCH = 16
SEQ = 256
DEPTH = 1
DEC_BATCH = 8
DEC_SEQ = 4096
PAST_LEN = 512

GRID_W = 64
F32 = jnp.float32
D_R = D_MODEL // 2
RWKV_HEAD = 64
H_R = D_R // RWKV_HEAD
W_LORA = 128
A_LORA = 128
G_LORA = 480
RWKV_SPLITS = (D_R, D_R, D_R, W_LORA, W_LORA, A_LORA, A_LORA, G_LORA)
RW_COLS = 3 * D_R + 2 * W_LORA + 2 * A_LORA + G_LORA
D_G = D_MODEL // 2
HGRN_EXPAND = 128
H_G = D_G // HGRN_EXPAND
HGRN_DV = D_G // H_G
HGRN_CHUNK = 64
HG_COLS = 5 * D_G
N_IN = RW_COLS + HG_COLS + 2 * D_MODEL
D_FF = 11008
FFN_HALF = 0.5
N_MOD = 9
RMS_EPS = 1e-6
GN_EPS = 64e-5

kernel_name = 'rwkv7_hgrn2_macaron_diffusion_step'


def _split(z, sizes):
    idx = []
    acc = 0
    for s in sizes[:-1]:
        acc += s
        idx.append(acc)
    return jnp.split(z, idx, axis=-1)


def rms_norm(x, gain):
    xf = x.astype(F32)
    y = xf * lax.rsqrt(jnp.mean(xf * xf, axis=-1, keepdims=True) + RMS_EPS)
    return (y * gain.astype(F32)).astype(x.dtype)


def swiglu(h, w_in, w_out):
    a, b = jnp.split(h @ w_in, 2, axis=-1)
    return (jax.nn.silu(a) * b) @ w_out


def centred_shift(z, mu_prev, mu_next):
    z_prev = jnp.pad(z[:, :-1], ((0, 0), (1, 0), (0, 0)))
    z_next = jnp.pad(z[:, 1:], ((0, 0), (0, 1), (0, 0)))
    return z + mu_prev * (z_prev - z) + mu_next * (z_next - z)


def rwkv7_scan(r, w, k, v, kk, a, s0, reverse):
    def step(s, inp):
        r_t, w_t, k_t, v_t, kk_t, a_t = inp
        sa = jnp.einsum('bhvk,bhk->bhv', s, -kk_t)
        s = (s * w_t[:, :, None, :]
             + sa[..., None] * (kk_t * a_t)[:, :, None, :]
             + v_t[..., None] * k_t[:, :, None, :])
        o = jnp.einsum('bhvk,bhk->bhv', s, r_t)
        return s, o
    xs = tuple(jnp.moveaxis(t, 1, 0) for t in (r, w, k, v, kk, a))
    s_fin, o = lax.scan(step, s0, xs, reverse=reverse)
    return jnp.moveaxis(o, 0, 1), s_fin


def rwkv7_branch(z, s0, mu_prev, mu_next, w0, w2, a0, a2, g2, k_k, k_a, r_k, ln_w, ln_b):
    B, T, _ = z.shape
    z = centred_shift(z.astype(F32), mu_prev.astype(F32), mu_next.astype(F32))
    r, k, v, wd_f, wd_b, ad_f, ad_b, gd = _split(z, RWKV_SPLITS)
    w0, w2, a0, a2, g2 = (t.astype(F32) for t in (w0, w2, a0, a2, g2))
    k_k, k_a, r_k = k_k.astype(F32), k_a.astype(F32), r_k.astype(F32)
    hd = lambda t: t.reshape(B, T, H_R, RWKV_HEAD)
    kk = hd(k * k_k)
    kk = kk / jnp.maximum(jnp.sqrt(jnp.sum(kk * kk, axis=-1, keepdims=True)), 1e-12)
    g = jax.nn.sigmoid(gd) @ g2
    s0 = s0.astype(F32)
    outs, bonuses, states = [], [], []
    for d, (wd, ad, rev) in enumerate(((wd_f, ad_f, False), (wd_b, ad_b, True))):
        w_log = -jax.nn.softplus(-(w0[d] + jnp.tanh(wd) @ w2[d])) - 0.5
        decay = jnp.exp(-jnp.exp(w_log))
        a = jax.nn.sigmoid(a0[d] + ad @ a2[d])
        k_d = k * (1.0 + (a - 1.0) * k_a)
        o, s = rwkv7_scan(hd(r), hd(decay), hd(k_d), hd(v), kk, hd(a), s0[:, d], rev)
        outs.append(o)
        bonuses.append(jnp.sum(hd(r) * hd(k_d) * r_k, axis=-1, keepdims=True) * hd(v))
        states.append(s)
    o = outs[0] + outs[1]
    mean = jnp.mean(o, axis=-1, keepdims=True)
    var = jnp.mean(jnp.square(o - mean), axis=-1, keepdims=True)
    o = ((o - mean) * lax.rsqrt(var + GN_EPS) * ln_w.astype(F32).reshape(H_R, RWKV_HEAD)
         + ln_b.astype(F32).reshape(H_R, RWKV_HEAD))
    y = (o + bonuses[0] + bonuses[1]).reshape(B, T, D_R) * g
    return y, jnp.stack(states, axis=1)


def hgrn2_chunk_scan(q, k, g, v, s0):
    B, T, H, _ = q.shape
    n = T // HGRN_CHUNK
    to_chunks = lambda t: jnp.moveaxis(t.reshape(B, n, HGRN_CHUNK, H, t.shape[-1]), 1, 0)
    causal = jnp.tril(jnp.ones((HGRN_CHUNK, HGRN_CHUNK), dtype=bool))

    def step(s, inp):
        q_c, k_c, g_c, v_c = inp
        b = jnp.cumsum(g_c, axis=1)
        diff = b[:, :, None] - b[:, None, :]
        decay = jnp.exp(jnp.where(causal[None, :, :, None, None], diff, -jnp.inf))
        attn = jnp.einsum('bthe,bshe,btshe->bhts', q_c, k_c, decay)
        o = (jnp.einsum('bthe,bhed->bthd', q_c * jnp.exp(b), s)
             + jnp.einsum('bhts,bshd->bthd', attn, v_c))
        b_last = b[:, -1]
        s = (jnp.exp(b_last)[..., None] * s
             + jnp.einsum('bshe,bshd->bhed', k_c * jnp.exp(b_last[:, None] - b), v_c))
        return s, o

    s_fin, o = lax.scan(step, s0, tuple(to_chunks(t) for t in (q, k, g, v)))
    return jnp.moveaxis(o, 0, 1).reshape(B, T, H, v.shape[-1]), s_fin


def hgrn2_branch(z, s0, lb, gain, is_latent):
    B, T, _ = z.shape
    q, f_f, f_b, i, gate = _split(z.astype(F32), (D_G,) * 5)
    if is_latent:
        rows = T // GRID_W
        to_col = lambda t: t.reshape(B, rows, GRID_W, t.shape[-1]).transpose(0, 2, 1, 3).reshape(B, T, t.shape[-1])
        q, f_f, f_b, i = to_col(q), to_col(f_f), to_col(f_b), to_col(i)
    q = jax.nn.silu(q).reshape(B, T, H_G, HGRN_EXPAND)
    v = i.reshape(B, T, H_G, HGRN_DV)
    lb_h = lb.astype(F32).reshape(H_G, HGRN_EXPAND)
    s0 = s0.astype(F32)
    outs, states = [], []
    for d, (f_pre, rev) in enumerate(((f_f, False), (f_b, True))):
        f = lb_h + (1.0 - lb_h) * jax.nn.sigmoid(f_pre.reshape(B, T, H_G, HGRN_EXPAND))
        args = (q, 1.0 - f, jnp.log(f), v)
        if rev:
            args = tuple(jnp.flip(t, axis=1) for t in args)
        o, s = hgrn2_chunk_scan(*args, s0[:, d])
        if rev:
            o = jnp.flip(o, axis=1)
        outs.append(o)
        states.append(s)
    o = outs[0] + outs[1]
    if is_latent:
        rows = T // GRID_W
        o = o.reshape(B, GRID_W, rows, H_G, HGRN_DV).transpose(0, 2, 1, 3, 4).reshape(B, T, H_G, HGRN_DV)
    o = o * lax.rsqrt(jnp.mean(o * o, axis=-1, keepdims=True) + RMS_EPS) * gain.astype(F32)
    y = o.reshape(B, T, D_G) * jax.nn.silu(gate)
    return y, jnp.stack(states, axis=1)


def mixer(h, s_r0, s_h0, P, l, lb, is_latent):
    z = h @ P['w_in'][l]
    z_r, z_h, z_g = _split(z, (RW_COLS, HG_COLS, 2 * D_MODEL))
    y_r, s_r = rwkv7_branch(z_r, s_r0, P['rwkv_mu_prev'][l], P['rwkv_mu_next'][l],
                            P['rwkv_w0'][l], P['rwkv_w2'][l], P['rwkv_a0'][l], P['rwkv_a2'][l],
                            P['rwkv_g2'][l], P['rwkv_k_k'][l], P['rwkv_k_a'][l], P['rwkv_r_k'][l],
                            P['rwkv_ln_w'][l], P['rwkv_ln_b'][l])
    y_h, s_h = hgrn2_branch(z_h, s_h0, lb, P['hgrn_norm'][l], is_latent)
    gate_r, gate_h = jnp.split(jax.nn.sigmoid(z_g.astype(F32)).astype(h.dtype), 2, axis=-1)
    merged = (gate_r * (y_r.astype(h.dtype) @ P['w_branch_rwkv'][l])
              + gate_h * (y_h.astype(h.dtype) @ P['w_branch_hgrn'][l]))
    return merged @ P['w_out'][l], s_r, s_h


def layer(x, cond, s_r0, s_h0, P, l, lb, is_latent):
    mod = jax.nn.silu(cond.astype(F32)) @ P['w_mod'][l] + P['b_mod'][l]
    sh1, sc1, g1, sh2, sc2, g2, sh3, sc3, g3 = jnp.split(mod[:, None, :].astype(x.dtype), N_MOD, axis=-1)
    h = rms_norm(x, P['norm_ffn1'][l]) * (1.0 + sc1) + sh1
    x = x + g1 * (FFN_HALF * swiglu(h, P['ffn1_w_in'][l], P['ffn1_w_out'][l]))
    h = rms_norm(x, P['norm_mix'][l]) * (1.0 + sc2) + sh2
    m, s_r, s_h = mixer(h, s_r0, s_h0, P, l, lb, is_latent)
    x = x + g2 * m
    h = rms_norm(x, P['norm_ffn2'][l]) * (1.0 + sc3) + sh3
    x = x + g3 * (FFN_HALF * swiglu(h, P['ffn2_w_in'][l], P['ffn2_w_out'][l]))
    return x, s_r, s_h


def setup_inputs(seed: int = 0) -> dict:
    key = jax.random.key(seed)
    ks = jax.random.split(key, 40)
    D = D_MODEL
    nrm = lambda k, shape, scale: jax.random.normal(k, shape, F32) * scale
    uni = lambda k, shape, lo, hi: jax.random.uniform(k, shape, F32, lo, hi)
    return {
        'x_prompt': nrm(ks[0], (BATCH, SEQ, D), 1.0),
        'x_sample': nrm(ks[1], (DEC_BATCH, DEC_SEQ, D), 1.0),
        'state_rwkv': nrm(ks[2], (DEC_BATCH, DEPTH, 2, H_R, RWKV_HEAD, RWKV_HEAD), 0.5),
        'state_hgrn': nrm(ks[3], (DEC_BATCH, DEPTH, 2, H_G, HGRN_EXPAND, HGRN_DV), 0.5),
        'c': nrm(ks[4], (DEC_BATCH, D), 1.0),
        'c_ctx': nrm(ks[5], (D,), 1.0),
        'w_mod': nrm(ks[6], (DEPTH, D, N_MOD * D), 0.5 * D ** -0.5),
        'b_mod': nrm(ks[7], (DEPTH, N_MOD * D), 0.01),
        'norm_ffn1': 1.0 + nrm(ks[8], (DEPTH, D), 0.02),
        'norm_mix': 1.0 + nrm(ks[9], (DEPTH, D), 0.02),
        'norm_ffn2': 1.0 + nrm(ks[10], (DEPTH, D), 0.02),
        'ffn1_w_in': nrm(ks[11], (DEPTH, D, 2 * D_FF), D ** -0.5),
        'ffn1_w_out': nrm(ks[12], (DEPTH, D_FF, D), D_FF ** -0.5),
        'ffn2_w_in': nrm(ks[13], (DEPTH, D, 2 * D_FF), D ** -0.5),
        'ffn2_w_out': nrm(ks[14], (DEPTH, D_FF, D), D_FF ** -0.5),
        'w_in': nrm(ks[15], (DEPTH, D, N_IN), D ** -0.5),
        'rwkv_mu_prev': uni(ks[16], (DEPTH, RW_COLS), 0.0, 0.5),
        'rwkv_mu_next': uni(ks[17], (DEPTH, RW_COLS), 0.0, 0.5),
        'rwkv_w0': uni(ks[18], (DEPTH, 2, D_R), -6.5, -1.5),
        'rwkv_w2': nrm(ks[19], (DEPTH, 2, W_LORA, D_R), 0.3 * W_LORA ** -0.5),
        'rwkv_a0': nrm(ks[20], (DEPTH, 2, D_R), 0.1),
        'rwkv_a2': nrm(ks[21], (DEPTH, 2, A_LORA, D_R), 0.5 * A_LORA ** -0.5),
        'rwkv_g2': nrm(ks[22], (DEPTH, G_LORA, D_R), G_LORA ** -0.5),
        'rwkv_k_k': 0.85 + nrm(ks[23], (DEPTH, D_R), 0.02),
        'rwkv_k_a': 1.0 + nrm(ks[24], (DEPTH, D_R), 0.02),
        'rwkv_r_k': nrm(ks[25], (DEPTH, H_R, RWKV_HEAD), 0.1),
        'rwkv_ln_w': 1.0 + nrm(ks[26], (DEPTH, D_R), 0.02),
        'rwkv_ln_b': nrm(ks[27], (DEPTH, D_R), 0.01),
        'hgrn_lb_logits': nrm(ks[28], (DEPTH + 1, D_G), 0.5),
        'hgrn_norm': 1.0 + nrm(ks[29], (DEPTH, HGRN_DV), 0.02),
        'w_branch_rwkv': nrm(ks[30], (DEPTH, D_R, D), D_R ** -0.5),
        'w_branch_hgrn': nrm(ks[31], (DEPTH, D_G, D), D_G ** -0.5),
        'w_out': nrm(ks[32], (DEPTH, D, D), D ** -0.5),
        'norm_final': 1.0 + nrm(ks[33], (D,), 0.02),
    }


def reference(x_prompt, x_sample, state_rwkv, state_hgrn, c, c_ctx, w_mod, b_mod,
              norm_ffn1, norm_mix, norm_ffn2, ffn1_w_in, ffn1_w_out, ffn2_w_in, ffn2_w_out,
              w_in, rwkv_mu_prev, rwkv_mu_next, rwkv_w0, rwkv_w2, rwkv_a0, rwkv_a2, rwkv_g2,
              rwkv_k_k, rwkv_k_a, rwkv_r_k, rwkv_ln_w, rwkv_ln_b, hgrn_lb_logits, hgrn_norm,
              w_branch_rwkv, w_branch_hgrn, w_out, norm_final):
    P = {
        'w_mod': w_mod, 'b_mod': b_mod, 'norm_ffn1': norm_ffn1, 'norm_mix': norm_mix,
        'norm_ffn2': norm_ffn2, 'ffn1_w_in': ffn1_w_in, 'ffn1_w_out': ffn1_w_out,
        'ffn2_w_in': ffn2_w_in, 'ffn2_w_out': ffn2_w_out, 'w_in': w_in,
        'rwkv_mu_prev': rwkv_mu_prev, 'rwkv_mu_next': rwkv_mu_next, 'rwkv_w0': rwkv_w0,
        'rwkv_w2': rwkv_w2, 'rwkv_a0': rwkv_a0, 'rwkv_a2': rwkv_a2, 'rwkv_g2': rwkv_g2,
        'rwkv_k_k': rwkv_k_k, 'rwkv_k_a': rwkv_k_a, 'rwkv_r_k': rwkv_r_k,
        'rwkv_ln_w': rwkv_ln_w, 'rwkv_ln_b': rwkv_ln_b, 'hgrn_norm': hgrn_norm,
        'w_branch_rwkv': w_branch_rwkv, 'w_branch_hgrn': w_branch_hgrn, 'w_out': w_out,
    }
    lb_all = jnp.cumsum(jax.nn.softmax(hgrn_lb_logits.astype(F32), axis=0), axis=0)

    nb = x_prompt.shape[0]
    zero_r = jnp.zeros((nb, 2, H_R, RWKV_HEAD, RWKV_HEAD), F32)
    zero_h = jnp.zeros((nb, 2, H_G, HGRN_EXPAND, HGRN_DV), F32)
    x = x_prompt
    new_r, new_h = [], []
    for l in range(DEPTH):
        x, s_r, s_h = layer(x, c_ctx[None, :], zero_r, zero_h, P, l, lb_all[l], False)
        new_r.append(s_r)
        new_h.append(s_h)
    y_prompt = rms_norm(x, norm_final)
    new_state_rwkv = jnp.stack(new_r, axis=1)
    new_state_hgrn = jnp.stack(new_h, axis=1)

    x = x_sample
    for l in range(DEPTH):
        x, _, _ = layer(x, c, state_rwkv[:, l], state_hgrn[:, l], P, l, lb_all[l], True)
    y_sample = rms_norm(x, norm_final)
    return (y_prompt, y_sample, new_state_rwkv, new_state_hgrn)
```

```python
import contextlib
import math

import numpy as np
import concourse.bass as bass
import concourse.mybir as mybir
from concourse.bass_utils import run_bass_kernel_spmd

F32 = mybir.dt.float32
BF16 = mybir.dt.bfloat16
I32 = mybir.dt.int32
AF = mybir.ActivationFunctionType
ALU = mybir.AluOpType
AX = mybir.AxisListType

SELF_SYNC = True
RMS_EPS = 1e-6
GN_EPS = 64e-5


class Cfg:
    def __init__(s, D=4096, DFF=11008, phases="ABC"):
        s.D = D
        s.DFF = DFF
        s.KC = D // 128
        s.DR = D // 2
        s.DG = D // 2
        s.HR = s.DR // 64
        s.NP = s.DR // 128
        s.HG = s.DG // 128
        s.RW = 3 * s.DR + 4 * 128 + 480
        s.HGC = 5 * s.DG
        s.NZ = s.RW + s.HGC
        s.NIN = s.NZ + 2 * D
        s.FC = DFF // 128
        assert DFF % 128 == 0 and D % 256 == 0
        s.T = 512
        s.NT = 9
        s.TOK = s.T * s.NT
        s.phases = phases
        npart = (s.FC + 21) // 22
        base = s.FC // npart
        rem = s.FC % npart
        s.parts = []
        o = 0
        for i in range(npart):
            n = base + (1 if i < rem else 0)
            s.parts.append((o, n))
            o += n
        s.GM = max(max(n for _, n in s.parts), s.KC)
        DR, DG = s.DR, s.DG
        s.c_r, s.c_k, s.c_v = 0, DR, 2 * DR
        s.c_wd = [3 * DR, 3 * DR + 128]
        s.c_ad = [3 * DR + 256, 3 * DR + 384]
        s.c_gd = 3 * DR + 512
        s.c_q = s.RW
        s.c_f = [s.RW + DG, s.RW + 2 * DG]
        s.c_i = s.RW + 3 * DG
        s.c_gate = s.RW + 4 * DG
        s.c_gr = s.NZ
        s.c_gh = s.NZ + D


class SemObj:
    def __init__(s, h, name):
        s.h = h
        s.n = 0
        s.name = name


class Eng:
    def __init__(s, name, q, sem):
        s.name = name
        s.q = q
        s.sem = sem
        s.seen = {}


class Tok:
    __slots__ = ("w", "r", "name")

    def __init__(s, name=""):
        s.w = None
        s.r = {}
        s.name = name


class Sched:
    def __init__(s, nc, stack):
        s.nc = nc
        s.stack = stack
        s.pe = Eng("pe", nc.tensor, s.newsem("s_pe"))
        s.act = Eng("act", nc.scalar, s.newsem("s_act"))
        s.dve = Eng("dve", nc.vector, s.newsem("s_dve"))
        s.pool = Eng("pool", nc.gpsimd, s.newsem("s_pool"))
        s.sp = Eng("sp", nc.sync, s.newsem("s_sp"))
        s.engs = [s.pe, s.act, s.dve, s.pool, s.sp]
        s.dsems = []
        s.nwait = 0

    def newsem(s, name):
        return SemObj(s.stack.enter_context(s.nc.semaphore(name)), name)

    def newdsem(s, name):
        d = s.newsem(name)
        s.dsems.append(d)
        return d

    def _wait(s, eng, so, v):
        if eng.seen.get(so, 0) < v:
            eng.q.wait_ge(so.h, v)
            eng.seen[so] = v
            s.nwait += 1

    def _deps(s, eng, reads, writes):
        evs = {}
        for t in reads:
            if t.w is not None:
                so, v = t.w
                if evs.get(so, 0) < v:
                    evs[so] = v
        for t in writes:
            if t.w is not None:
                so, v = t.w
                if evs.get(so, 0) < v:
                    evs[so] = v
            for so, v in t.r.items():
                if evs.get(so, 0) < v:
                    evs[so] = v
        for so, v in evs.items():
            if so is eng.sem and (eng.name == "pe" or not SELF_SYNC):
                continue
            s._wait(eng, so, v)

    def op(s, eng, emit, reads=(), writes=(), signal=True):
        s._deps(eng, reads, writes)
        ins = emit()
        if signal:
            ins.then_inc(eng.sem.h, 1)
            eng.sem.n += 1
            ev = (eng.sem, eng.sem.n)
        else:
            ev = (eng.sem, eng.sem.n + 1)
        for t in reads:
            if t.r.get(ev[0], 0) < ev[1]:
                t.r[ev[0]] = ev[1]
        for t in writes:
            t.w = ev
            t.r = {}
        return ins

    def dma(s, eng, emit, dsem, reads=(), writes=()):
        s._deps(eng, reads, writes)
        ins = emit()
        ins.then_inc(dsem.h, 16)
        dsem.n += 16
        ev = (dsem, dsem.n)
        for t in reads:
            if t.r.get(ev[0], 0) < ev[1]:
                t.r[ev[0]] = ev[1]
        for t in writes:
            t.w = ev
            t.r = {}
        return ins

    def barrier(s):
        for e in s.engs:
            for o in s.engs:
                if o is not e and o.sem.n > 0:
                    s._wait(e, o.sem, o.sem.n)
            for d in s.dsems:
                if d.n > 0:
                    s._wait(e, d, d.n)


class Ring:
    def __init__(s, S, aps, name, dsem=False):
        s.aps = aps
        s.toks = [Tok(f"{name}{i}") for i in range(len(aps))]
        s.dsems = [S.newdsem(f"d_{name}{i}") for i in range(len(aps))] if dsem else None
        s.i = 0

    def next(s):
        i = s.i
        s.i = (s.i + 1) % len(s.aps)
        if s.dsems:
            return s.aps[i], s.toks[i], s.dsems[i]
        return s.aps[i], s.toks[i]


class Arena:
    def __init__(s, ap, nelem):
        s.ap = ap
        s.n = nelem
        s.off = 0

    def reset(s):
        s.off = 0

    def alloc(s, shape, dt=F32):
        shape = list(shape)
        free = 1
        for d in shape[1:]:
            free *= d
        nb = free * (2 if dt == F32 else 1)
        nb = (nb + 15) // 16 * 16
        assert s.off + nb <= s.n, f"arena overflow: need {s.off + nb} have {s.n}"
        v = s.ap[0:shape[0], s.off:s.off + (free * (2 if dt == F32 else 1))]
        s.off += nb
        if dt == F32:
            v = v.bitcast(F32)
        if len(shape) > 2:
            names = [f"d{i}" for i in range(len(shape) - 1)]
            kw = {names[i]: shape[i + 1] for i in range(len(shape) - 1)}
            v = v.rearrange("p (" + " ".join(names) + ") -> p " + " ".join(names), **kw)
        return v

def build(cfg):
    nc = bass.Bass("TRN2", target_bir_lowering=False)
    D, KC, T, TOK, DFF = cfg.D, cfg.KC, cfg.T, cfg.TOK, cfg.DFF
    SEG = 512
    SK = SEG // 128
    DR, DG, NP, HG, HR = cfg.DR, cfg.DG, cfg.NP, cfg.HG, cfg.HR

    def din(name, shape, dt=F32):
        return nc.dram_tensor(name, list(shape), dt, kind="ExternalInput").ap()

    def dout(name, shape, dt=F32):
        return nc.dram_tensor(name, list(shape), dt, kind="ExternalOutput").ap()

    def dscr(name, shape, dt=F32):
        return nc.dram_tensor(name, list(shape), dt, kind="Internal").ap()

    x_in = din("x_tok", [TOK, D])
    cond_in = din("cond", [2, D])
    st_r_in = din("state_rwkv", [2, HR, 64, 64])
    st_h_in = din("state_hgrn", [2, HG, 128, 128])
    w_mod = din("w_mod", [D, 9 * D])
    b_mod = din("b_mod", [9 * D])
    norm_ffn1 = din("norm_ffn1", [D])
    norm_mix = din("norm_mix", [D])
    norm_ffn2 = din("norm_ffn2", [D])
    ffn1_w_in = din("ffn1_w_in", [D, 2 * DFF])
    ffn1_w_out = din("ffn1_w_out", [DFF, D])
    ffn2_w_in = din("ffn2_w_in", [D, 2 * DFF])
    ffn2_w_out = din("ffn2_w_out", [DFF, D])
    w_in = din("w_in", [D, cfg.NIN])
    mu_prev = din("rwkv_mu_prev", [cfg.RW])
    mu_next = din("rwkv_mu_next", [cfg.RW])
    rw_w0 = din("rwkv_w0", [2, DR])
    rw_w2 = din("rwkv_w2", [2, 128, DR])
    rw_a0 = din("rwkv_a0", [2, DR])
    rw_a2 = din("rwkv_a2", [2, 128, DR])
    rw_g2 = din("rwkv_g2", [480, DR])
    rw_kk = din("rwkv_k_k", [DR])
    rw_ka = din("rwkv_k_a", [DR])
    rw_rk = din("rwkv_r_k", [DR])
    rw_lnw = din("rwkv_ln_w", [DR])
    rw_lnb = din("rwkv_ln_b", [DR])
    lb_logits = din("hgrn_lb_logits", [2, DG])
    hg_norm = din("hgrn_norm", [128])
    w_br = din("w_branch_rwkv", [DR, D])
    w_bh = din("w_branch_hgrn", [DG, D])
    w_out = din("w_out", [D, D])
    norm_final = din("norm_final", [D])

    y_out = dout("y_tok", [TOK, D])
    nsr_out = dout("ns_rwkv", [2, 2, HR, 64, 64])
    nsh_out = dout("ns_hgrn", [2, 2, HG, 128, 128])

    zTr = dscr("zTr", [cfg.RW, TOK])
    zTh = dscr("zTh", [cfg.HGC, TOK])
    x1T = dscr("x1T", [D, TOK])
    ofT = dscr("ofT", [DR, TOK])
    yrT = dscr("yrT", [DR, TOK], BF16)
    yhT = dscr("yhT", [DG, TOK], BF16)

    stack = contextlib.ExitStack()
    with stack:
        S = Sched(nc, stack)
        PE, ACT, DVE, POOL, SP = S.pe, S.act, S.dve, S.pool, S.sp

        def sb(name, shape, dt=F32, st=None):
            return (st or stack).enter_context(nc.sbuf_tensor(name, list(shape), dt))

        def pst(name, shape, dt=F32, st=None):
            return (st or stack).enter_context(nc.psum_tensor(name, list(shape), dt))

        ident = sb("ident", [128, 128])
        identb = sb("identb", [128, 128], BF16)
        ones = sb("ones", [128, 128])
        iot = sb("iot", [128, 128])
        epsv = sb("epsv", [128, 1])
        t_const = Tok("const")
        S.op(POOL, lambda: nc.gpsimd.iota(iot[:], pattern=[[1, 128]], base=0, channel_multiplier=-1,
                                          allow_small_or_imprecise_dtypes=True), writes=[t_const])
        S.op(DVE, lambda: nc.vector.tensor_single_scalar(ident[:], iot[:], 0.0, op=ALU.is_equal),
             reads=[t_const], writes=[t_const])
        S.op(DVE, lambda: nc.vector.tensor_copy(identb[:], ident[:]), reads=[t_const], writes=[t_const])
        S.op(DVE, lambda: nc.vector.memset(ones[:], 1.0), writes=[t_const])
        S.op(DVE, lambda: nc.vector.memset(epsv[:], RMS_EPS), writes=[t_const])

        psum = [pst(f"ps{i}", [128, 512]) for i in range(8)]
        PS = Ring(S, [p[:] for p in psum], "ps")

        d_misc = S.newdsem("d_misc")
        t_misc = Tok("misc")

        modc = sb("modc", [128, 2, 9 * KC])
        nrm = sb("nrm", [128, 4, KC])
        Acoef = sb("Acoef", [128, 2, 3, KC])
        Gcoef = sb("Gcoef", [128, 2, 3, KC])
        t_mod = Tok("mod")

        xst = sb("xst", [128, 2, SEG], F32)
        XST = Ring(S, [xst[:, i, :] for i in range(2)], "xst", dsem=True)
        mU_i = sb("mU_i", [128, 128]); mU_s = sb("mU_s", [128, 128]); mL_i = sb("mL_i", [128, 128]); mL_s = sb("mL_s", [128, 128])
        for m_, op_ in ((mU_i, ALU.is_ge), (mU_s, ALU.is_gt), (mL_i, ALU.is_le), (mL_s, ALU.is_lt)):
            S.op(DVE, lambda m_=m_, op_=op_: nc.vector.tensor_single_scalar(m_[:], iot[:], 0.0, op=op_),
                 reads=[t_const], writes=[t_const])

        ARENA_BF = (nc.sbuf_bytes_remaining - 64) // 32 * 16
        arena_t = sb("arena", [128, ARENA_BF], BF16)
        AR = Arena(arena_t[:], ARENA_BF)
        xres = AR.alloc([128, KC, T], F32)
        hbuf = AR.alloc([128, KC, T], BF16)
        gm = AR.alloc([128, cfg.GM, T], BF16)
        scr = AR.alloc([128, max(2 * NP * T, 6 * T)], BF16)
        yrh = scr[:, 0:2 * NP * T].rearrange("p (a k t) -> p a k t", a=2, k=NP)
        NSLOT = 3
        wsl = AR.alloc([128, NSLOT, 4096], BF16)
        WS = Ring(S, [wsl[:, i, :] for i in range(NSLOT)], "w", dsem=True)
        tmpf = AR.alloc([128, 4, T], F32)
        TMP = Ring(S, [tmpf[:, i, :] for i in range(4)], "tmp")
        rstd = AR.alloc([128, T], F32)
        t_rstd = Tok("rstd")
        zst = scr[:, 0:6 * T].bitcast(F32).rearrange("p (i t) -> p i t", i=3)
        ZST = Ring(S, [zst[:, i, :] for i in range(3)], "zst", dsem=True)
        t_x = Tok("x")
        t_h = Tok("h")
        t_gm = Tok("gm")
        t_yrh = Tok("yrh")
        d_x = S.newdsem("d_x")
        d_y = S.newdsem("d_y")

        def load_w(W, r0, kc_n, c0, ncols, rows_last=128):
            ap, tok, dsem = WS.next()
            view = ap[:, 0:kc_n * ncols].rearrange("p (k n) -> p k n", n=ncols)
            if rows_last == 128:
                src = W[r0:r0 + kc_n * 128, c0:c0 + ncols].rearrange("(k p) n -> p k n", p=128)
                S.dma(POOL, lambda: nc.gpsimd.dma_start(out=view, in_=src), dsem, writes=[tok])
            else:
                if kc_n > 1:
                    src = W[r0:r0 + (kc_n - 1) * 128, c0:c0 + ncols].rearrange("(k p) n -> p k n", p=128)
                    S.dma(POOL, lambda: nc.gpsimd.dma_start(out=view[:, 0:kc_n - 1, :], in_=src), dsem,
                          writes=[tok])
                src2 = W[r0 + (kc_n - 1) * 128:r0 + (kc_n - 1) * 128 + rows_last, c0:c0 + ncols]
                S.dma(POOL, lambda: nc.gpsimd.dma_start(out=view[0:rows_last, kc_n - 1, :], in_=src2), dsem,
                      writes=[tok])
            return view, tok

        def mm_group(ps_ap, ps_tok, items):
            n = len(items)
            for i, (l, r, toks) in enumerate(items):
                S.op(PE, lambda l=l, r=r, i=i: nc.tensor.matmul(ps_ap, l, r, start=(i == 0), stop=(i == n - 1)),
                     reads=toks, writes=[ps_tok], signal=(i == n - 1))

        def transpose_to(ps_ap, ps_tok, in_ap, in_toks, idn, signal=True, extra_w=()):
            S.op(PE, lambda: nc.tensor.transpose(ps_ap, in_ap, idn), reads=list(in_toks) + [t_const],
                 writes=[ps_tok] + list(extra_w), signal=signal)

        def load_vec_fm(dst, vec, n, eng=None):
            nch = (n + 127) // 128
            full = n // 128
            c = 0
            while c < nch:
                g = min(96, nch - c)
                st_ap, st_tok, st_d = XST.next()
                gf = min(g, full - c) if full > c else 0
                stv = st_ap[0:g, 0:128]
                if g > gf:
                    S.op(DVE, lambda stv=stv: nc.vector.memset(stv, 0.0), writes=[st_tok])
                if gf > 0:
                    src = vec[c * 128:(c + gf) * 128].rearrange("(k p) -> k p", p=128)
                    S.dma(SP, lambda src=src, o=st_ap[0:gf, 0:128]: nc.sync.dma_start(out=o, in_=src), st_d,
                          writes=[st_tok])
                if g > gf:
                    rem = n - full * 128
                    src = vec[full * 128:n].rearrange("(k p) -> k p", k=1)
                    S.dma(SP, lambda src=src, o=st_ap[gf:gf + 1, 0:rem]: nc.sync.dma_start(out=o, in_=src), st_d,
                          writes=[st_tok])
                ps_ap, ps_tok = PS.next()
                transpose_to(ps_ap[:, 0:g], ps_tok, stv, [st_tok], ident[0:g, 0:g])
                S.op(DVE, lambda o=dst[:, c:c + g], i=ps_ap[:, 0:g]: nc.vector.tensor_copy(o, i),
                     reads=[ps_tok], writes=[t_misc])
                c += g

        def phase0():
            for i, v in enumerate((norm_ffn1, norm_mix, norm_ffn2, norm_final)):
                load_vec_fm(nrm[:, i, :], v, D)
            load_vec_fm(modc[:, 0, :], b_mod, 9 * D)
            st_ap, st_tok, st_d = XST.next()
            nseg = (D + SEG - 1) // SEG
            sT = AR.alloc([128, KC, 2], BF16)
            t_sT = Tok("sT")
            for sg in range(nseg):
                w = min(SEG, D - sg * SEG)
                if sg > 0:
                    st_ap, st_tok, st_d = XST.next()
                S.dma(SP, lambda o=st_ap[0:2, 0:w], i=cond_in[:, sg * SEG:sg * SEG + w]: nc.sync.dma_start(out=o, in_=i),
                      st_d, writes=[st_tok])
                S.op(ACT, lambda a=st_ap[0:2, 0:w]: nc.scalar.activation(a, a, AF.Silu), reads=[st_tok], writes=[st_tok])
                ps_ap, ps_tok = PS.next()
                nk = w // 128
                for k in range(nk):
                    transpose_to(ps_ap[:, 2 * k:2 * k + 2], ps_tok, st_ap[0:2, k * 128:(k + 1) * 128], [st_tok],
                                 ident[0:2, 0:2], signal=(k == nk - 1))
                S.op(DVE, lambda o=sT[:, sg * SK:sg * SK + nk, :], i=ps_ap[:, 0:2 * nk].rearrange("p (k c) -> p k c", c=2):
                     nc.vector.tensor_copy(o, i), reads=[ps_tok], writes=[t_sT])
            nout = 9 * KC
            per_bank = 256
            for o0 in range(0, nout, per_bank):
                ps_ap, ps_tok = PS.next()
                no = min(per_bank, nout - o0)
                for oi in range(no):
                    oc = o0 + oi
                    wv, wt = load_w(w_mod, 0, KC, oc * 128, 128)
                    mm_group(ps_ap[:, 2 * oi:2 * oi + 2], ps_tok,
                             [(wv[:, k, :], sT[:, k, :], [wt, t_sT]) for k in range(KC)])
                pv = ps_ap[:, 0:2 * no].rearrange("p (o c) -> p o c", c=2)
                for c in (1, 0):
                    S.op(DVE, lambda c=c, pv=pv, o0=o0, no=no: nc.vector.tensor_tensor(
                        modc[:, c, o0:o0 + no], pv[:, :, c], modc[:, 0, o0:o0 + no], op=ALU.add),
                        reads=[ps_tok, t_misc], writes=[t_mod] if c == 1 else [t_mod, t_misc])
            for c in range(2):
                for sl in range(3):
                    sc = modc[:, c, (3 * sl + 1) * KC:(3 * sl + 2) * KC]
                    gt = modc[:, c, (3 * sl + 2) * KC:(3 * sl + 3) * KC]
                    S.op(DVE, lambda sc=sc, c=c, sl=sl: nc.vector.scalar_tensor_tensor(
                        Acoef[:, c, sl, :], sc, 1.0, nrm[:, sl, :], op0=ALU.add, op1=ALU.mult),
                        reads=[t_mod, t_misc], writes=[t_mod])
                    S.op(DVE, lambda gt=gt, c=c, sl=sl: nc.vector.tensor_scalar(
                        Gcoef[:, c, sl, :], gt, 0.5 if sl != 1 else 1.0, None, op0=ALU.mult),
                        reads=[t_mod], writes=[t_mod])

        def rms_stats():
            ps_ap, ps_tok = PS.next()
            for k in range(KC):
                tp, tt = TMP.next()
                S.op(ACT, lambda tp=tp, k=k: nc.scalar.activation(tp, xres[:, k, :], AF.Square), reads=[t_x], writes=[tt])
                S.op(PE, lambda tp=tp, k=k: nc.tensor.matmul(ps_ap, ones[:], tp, start=(k == 0), stop=(k == KC - 1)),
                     reads=[tt, t_const], writes=[ps_tok], signal=True)
            S.op(ACT, lambda: nc.scalar.activation(rstd[:], ps_ap, AF.Sqrt, bias=epsv[:], scale=1.0 / D),
                 reads=[ps_tok, t_const], writes=[t_rstd])
            S.op(DVE, lambda: nc.vector.reciprocal(rstd[:], rstd[:]), reads=[t_rstd], writes=[t_rstd])

        def norm_mod(c, sl):
            rms_stats()
            for k in range(KC):
                tp, tt = TMP.next()
                S.op(DVE, lambda tp=tp, k=k: nc.vector.scalar_tensor_tensor(
                    tp, xres[:, k, :], Acoef[:, c, sl, k:k + 1], rstd[:], op0=ALU.mult, op1=ALU.mult),
                    reads=[t_x, t_rstd, t_mod], writes=[tt])
                S.op(ACT, lambda tp=tp, k=k: nc.scalar.activation(
                    hbuf[:, k, :], tp, AF.Identity, bias=modc[:, c, 3 * sl * KC + k:3 * sl * KC + k + 1], scale=1.0),
                    reads=[tt, t_mod], writes=[t_h])

        def mm2(W, r0, KCn, c0, nch, rhs_of_k, rtoks):
            ncols = 128 * nch
            banks = [PS.next() for _ in range(nch)]
            kg = min(KCn, 4096 // ncols)
            k0 = 0
            while k0 < KCn:
                kn = min(kg, KCn - k0)
                wv, wt = load_w(W, r0 + k0 * 128, kn, c0, ncols)
                for ci in range(nch):
                    pa, pta = banks[ci]
                    for k in range(kn):
                        kk_ = k0 + k
                        S.op(PE, lambda pa=pa, wv=wv, k=k, ci=ci, kk_=kk_: nc.tensor.matmul(
                            pa, wv[:, k, ci * 128:(ci + 1) * 128], rhs_of_k(kk_), start=(kk_ == 0), stop=(kk_ == KCn - 1)),
                            reads=[wt] + rtoks, writes=[pta], signal=(k == kn - 1))
                k0 += kn
            return banks

        def ffn(c, sl, Win, Wout):
            for (f0, fn) in cfg.parts:
                fi = 0
                while fi < fn:
                    nch = 2 if fi + 1 < fn else 1
                    f = f0 + fi
                    ba = mm2(Win, 0, KC, f * 128, nch, lambda k: hbuf[:, k, :], [t_h])
                    bb = mm2(Win, 0, KC, DFF + f * 128, nch, lambda k: hbuf[:, k, :], [t_h])
                    for ci in range(nch):
                        (pa, pta), (pb, ptb) = ba[ci], bb[ci]
                        tp, tt = TMP.next()
                        S.op(ACT, lambda tp=tp, pa=pa: nc.scalar.activation(tp, pa, AF.Silu), reads=[pta], writes=[tt])
                        S.op(DVE, lambda tp=tp, pb=pb, fi=fi, ci=ci: nc.vector.tensor_tensor(gm[:, fi + ci, :], tp, pb, op=ALU.mult),
                             reads=[tt, ptb], writes=[t_gm])
                    fi += nch
                for oc in range(0, KC, 2):
                    bo = mm2(Wout, f0 * 128, fn, oc * 128, 2, lambda k: gm[:, k, :], [t_gm])
                    for ci in range(2):
                        po, pto = bo[ci]
                        S.op(DVE, lambda po=po, o_=oc + ci: nc.vector.scalar_tensor_tensor(
                            xres[:, o_, :], po, Gcoef[:, c, sl, o_:o_ + 1], xres[:, o_, :], op0=ALU.mult, op1=ALU.add),
                            reads=[pto, t_mod, t_x], writes=[t_x])

        def load_x_tokmajor(tok0):
            nseg = (D + SEG - 1) // SEG
            for tb in range(T // 128):
                for sg in range(nseg):
                    w = min(SEG, D - sg * SEG)
                    st_ap, st_tok, st_d = XST.next()
                    S.dma(SP, lambda o=st_ap[:, 0:w], i=x_in[tok0 + tb * 128:tok0 + (tb + 1) * 128, sg * SEG:sg * SEG + w]:
                          nc.sync.dma_start(out=o, in_=i), st_d, writes=[st_tok])
                    for q in range(0, w // 128, 4):
                        nq = min(4, w // 128 - q)
                        ps_ap, ps_tok = PS.next()
                        for j in range(nq):
                            transpose_to(ps_ap[:, j * 128:(j + 1) * 128], ps_tok, st_ap[:, (q + j) * 128:(q + j + 1) * 128],
                                         [st_tok], ident[:], signal=(j == nq - 1))
                        k0 = sg * SK + q
                        S.op(DVE, lambda ps_ap=ps_ap, k0=k0, nq=nq, tb=tb: nc.vector.tensor_copy(
                            xres[:, k0:k0 + nq, tb * 128:(tb + 1) * 128],
                            ps_ap[:, 0:nq * 128].rearrange("p (k t) -> p k t", t=128)),
                            reads=[ps_tok], writes=[t_x])

        def store_y_tokmajor(tok0):
            rms_stats()
            nseg = (D + SEG - 1) // SEG
            for tb in range(T // 128):
                for sg in range(nseg):
                    w = min(SEG, D - sg * SEG)
                    st_ap, st_tok, st_d = XST.next()
                    for q in range(0, w // 128, 4):
                        nq = min(4, w // 128 - q)
                        ps_ap, ps_tok = PS.next()
                        for j in range(nq):
                            k = sg * SK + q + j
                            tp, tt = TMP.next()
                            S.op(DVE, lambda tp=tp, k=k, tb=tb: nc.vector.scalar_tensor_tensor(
                                tp[:, 0:128], xres[:, k, tb * 128:(tb + 1) * 128], nrm[:, 3, k:k + 1],
                                rstd[:, tb * 128:(tb + 1) * 128], op0=ALU.mult, op1=ALU.mult),
                                reads=[t_x, t_rstd, t_misc], writes=[tt])
                            transpose_to(ps_ap[:, j * 128:(j + 1) * 128], ps_tok, tp[:, 0:128], [tt], ident[:],
                                         signal=(j == nq - 1))
                        S.op(ACT, lambda ps_ap=ps_ap, st_ap=st_ap, q=q, nq=nq: nc.scalar.copy(
                            st_ap[:, q * 128:(q + nq) * 128], ps_ap[:, 0:nq * 128]), reads=[ps_tok], writes=[st_tok])
                    S.dma(SP, lambda i=st_ap[:, 0:w], o=y_out[tok0 + tb * 128:tok0 + (tb + 1) * 128, sg * SEG:sg * SEG + w]:
                          nc.sync.dma_start(out=o, in_=i), st_d, reads=[st_tok])

        def zchunks():
            out = []
            c = 0
            while c < cfg.RW:
                n = min(128, cfg.RW - c)
                out.append((c, n))
                c += n
            while c < cfg.NZ:
                out.append((c, 128))
                c += 128
            return out

        def phaseA(tt_i):
            c = 0 if tt_i == 0 else 1
            tok0 = tt_i * T
            load_x_tokmajor(tok0)
            norm_mod(c, 0)
            ffn(c, 0, ffn1_w_in, ffn1_w_out)
            S.dma(SP, lambda: nc.sync.dma_start(out=x1T.rearrange("(k p) t -> p k t", p=128)[:, :, tok0:tok0 + T], in_=xres[:]),
                  d_x, reads=[t_x])
            norm_mod(c, 1)
            zc = zchunks()
            zi = 0
            while zi < len(zc):
                c0, n = zc[zi]
                if n == 128 and zi + 1 < len(zc) and zc[zi + 1] == (c0 + 128, 128):
                    nch = 2
                    banks = mm2(w_in, 0, KC, c0, 2, lambda k: hbuf[:, k, :], [t_h])
                else:
                    nch = 1
                    wv, wt = load_w(w_in, 0, KC, c0, n)
                    pz, ptz = PS.next()
                    mm_group(pz[0:n, :], ptz, [(wv[:, k, :], hbuf[:, k, :], [wt, t_h]) for k in range(KC)])
                    banks = [(pz, ptz)]
                for ci in range(nch):
                    cc = c0 + ci * 128
                    pz, ptz = banks[ci]
                    zs, zt, zd = ZST.next()
                    S.op(ACT, lambda zs=zs, pz=pz, n=n: nc.scalar.copy(zs[0:n, :], pz[0:n, :]), reads=[ptz], writes=[zt])
                    zdst = zTr[cc:cc + n, tok0:tok0 + T] if cc < cfg.RW else zTh[cc - cfg.RW:cc - cfg.RW + n, tok0:tok0 + T]
                    S.dma(SP, lambda zs=zs, zdst=zdst, n=n: nc.sync.dma_start(out=zdst, in_=zs[0:n, :]), zd, reads=[zt])
                zi += nch

        def phaseC(tt_i):
            c = 0 if tt_i == 0 else 1
            tok0 = tt_i * T
            S.dma(SP, lambda: nc.sync.dma_start(out=xres[:], in_=x1T.rearrange("(k p) t -> p k t", p=128)[:, :, tok0:tok0 + T]),
                  d_x, writes=[t_x])
            tm = getattr(cfg, "test_mode", "")
            if tm == "dense":
                S.op(DVE, lambda: nc.vector.memset(yrh[:], 0.0), writes=[t_yrh])
            else:
                if tm == "hgrn":
                    S.op(DVE, lambda: nc.vector.memset(yrh[:, 0], 0.0), writes=[t_yrh])
                else:
                    S.dma(SP, lambda: nc.sync.dma_start(out=yrh[:, 0], in_=yrT.rearrange("(k p) t -> p k t", p=128)[:, :, tok0:tok0 + T]),
                          d_y, writes=[t_yrh])
                S.dma(SP, lambda: nc.sync.dma_start(out=yrh[:, 1], in_=yhT.rearrange("(k p) t -> p k t", p=128)[:, :, tok0:tok0 + T]),
                      d_y, writes=[t_yrh])
            norm_mod(c, 1)
            for oc in range(0, KC, 2):
                b1 = mm2(w_in, 0, KC, cfg.c_gr + oc * 128, 2, lambda k: hbuf[:, k, :], [t_h])
                b2 = mm2(w_br, 0, NP, oc * 128, 2, lambda k: yrh[:, 0, k, :], [t_yrh])
                tas = []
                for ci in range(2):
                    (p1, pt1), (p2, pt2) = b1[ci], b2[ci]
                    ta, tta = TMP.next()
                    S.op(ACT, lambda ta=ta, p1=p1: nc.scalar.activation(ta, p1, AF.Sigmoid), reads=[pt1], writes=[tta])
                    S.op(DVE, lambda ta=ta, p2=p2: nc.vector.tensor_tensor(ta, ta, p2, op=ALU.mult), reads=[tta, pt2], writes=[tta])
                    tas.append((ta, tta))
                b3 = mm2(w_in, 0, KC, cfg.c_gh + oc * 128, 2, lambda k: hbuf[:, k, :], [t_h])
                b4 = mm2(w_bh, 0, NP, oc * 128, 2, lambda k: yrh[:, 1, k, :], [t_yrh])
                for ci in range(2):
                    (p3, pt3), (p4, pt4) = b3[ci], b4[ci]
                    ta, tta = tas[ci]
                    tb_, ttb = TMP.next()
                    S.op(ACT, lambda tb_=tb_, p3=p3: nc.scalar.activation(tb_, p3, AF.Sigmoid), reads=[pt3], writes=[ttb])
                    S.op(DVE, lambda tb_=tb_, p4=p4: nc.vector.tensor_tensor(tb_, tb_, p4, op=ALU.mult), reads=[ttb, pt4], writes=[ttb])
                    S.op(DVE, lambda ta=ta, tb_=tb_, o_=oc + ci: nc.vector.tensor_tensor(gm[:, o_, :], ta, tb_, op=ALU.add),
                         reads=[tta, ttb], writes=[t_gm])
            for oc in range(0, KC, 2):
                bo = mm2(w_out, 0, KC, oc * 128, 2, lambda k: gm[:, k, :], [t_gm])
                for ci in range(2):
                    po, pto = bo[ci]
                    S.op(DVE, lambda po=po, o_=oc + ci: nc.vector.scalar_tensor_tensor(
                        xres[:, o_, :], po, Gcoef[:, c, 1, o_:o_ + 1], xres[:, o_, :], op0=ALU.mult, op1=ALU.add),
                        reads=[pto, t_mod, t_x], writes=[t_x])
            norm_mod(c, 2)
            ffn(c, 2, ffn2_w_in, ffn2_w_out)
            store_y_tokmajor(tok0)


        def bc(ap, shape):
            return ap.to_broadcast(list(shape))

        def phaseB_hgrn():
            AR.reset()
            LMAX = 4096
            P = AR.alloc([128, 6, LMAX], F32)
            Pflat = P[:].rearrange("p a l -> p (a l)")
            tP = [Tok(f"hP{i}") for i in range(6)]
            dP = [S.newdsem(f"d_hP{i}") for i in range(6)]
            qt = AR.alloc([128, 2, LMAX], BF16)
            kt = AR.alloc([128, 2, LMAX], BF16)
            t_qt = [Tok("qt0"), Tok("qt1")]
            t_kt = [Tok("kt0"), Tok("kt1")]
            Vt = AR.alloc([64, LMAX // 64, 128], BF16)
            t_Vt = Tok("Vt")
            rmask = AR.alloc([128, LMAX], BF16)
            yT = AR.alloc([128, LMAX], BF16)
            t_yT = Tok("yT")
            d_yT = S.newdsem("d_hyT")
            EM = AR.alloc([128, 2, 64], F32)
            ET = AR.alloc([128, 2, 64], F32)
            EE = AR.alloc([128, 2, 64], F32)
            tmpv = AR.alloc([128, 64], F32)
            t_E = [Tok("E0"), Tok("E1")]
            Sst = AR.alloc([128, 2, 128], F32)
            t_S = [Tok("S0"), Tok("S1")]
            d_S = [S.newdsem("d_hS0"), S.newdsem("d_hS1")]
            smr = AR.alloc([128, 4, 128], BF16)
            SMR = Ring(S, [smr[:, i, :] for i in range(4)], "sm")
            atr = AR.alloc([64, 4, 64], BF16)
            ATR = Ring(S, [atr[:, i, :] for i in range(4)], "at")
            str_ = AR.alloc([128, 4, 128], F32)
            STR = Ring(S, [str_[:, i, :] for i in range(4)], "stt")
            LBc = AR.alloc([128, 4, HG], F32)
            gainv = AR.alloc([128, 1], F32)
            ssb = AR.alloc([64, 64], F32)
            t_ss = Tok("ss")
            load_vec_fm(LBc[:, 0, :], lb_logits[0, :], DG)
            load_vec_fm(LBc[:, 1, :], lb_logits[1, :], DG)
            load_vec_fm(gainv[:, 0:1], hg_norm, 128)
            S.op(DVE, lambda: nc.vector.tensor_tensor(LBc[:, 2, :], LBc[:, 0, :], LBc[:, 1, :], op=ALU.subtract),
                 reads=[t_misc], writes=[t_misc])
            S.op(ACT, lambda: nc.scalar.activation(LBc[:, 2, :], LBc[:, 2, :], AF.Sigmoid), reads=[t_misc], writes=[t_misc])
            S.op(DVE, lambda: nc.vector.tensor_scalar(LBc[:, 3, :], LBc[:, 2, :], -1.0, 1.0, op0=ALU.mult, op1=ALU.add),
                 reads=[t_misc], writes=[t_misc])
            S.op(DVE, lambda: nc.vector.memset(rmask[:], 1.0), writes=[t_misc])
            S.op(DVE, lambda: nc.vector.memset(rmask[:].rearrange("p (n i) -> p n i", i=64)[:, :, 0:1], 0.0), writes=[t_misc])

            def seq(tok0, L, latent, sidx):
                nch = L // 64

                def cv(ap):
                    return ap.rearrange("p (n i) -> p n i", i=64)

                def pv(ap):
                    return ap.rearrange("p (i n) -> p n i", n=64) if latent else cv(ap)

                Pb = [P[:, i, 0:L] for i in range(6)]
                o_acc = Pflat[0:64, 0:2 * L].rearrange("p (n v) -> p n v", v=128)
                t_oacc = [tP[0], tP[1]]
                ktT = [P[:, 5, :].bitcast(BF16)[0:64, 0:nch * 128].rearrange("p (n e) -> p n e", e=128),
                       P[:, 2, :].bitcast(BF16)[0:64, 0:nch * 128].rearrange("p (n e) -> p n e", e=128)]
                t_ktT = [tP[5], tP[2]]
                onb = Pflat[0:64, 2 * LMAX:2 * LMAX + 2 * L].rearrange("p (n v) -> p n v", v=128)
                t_onb = [tP[2], tP[3]]
                psb_view = lambda ps_ap: ps_ap.bitcast(BF16)
                for hd in range(HG):
                    rows = lambda base: zTh[base + hd * 128: base + (hd + 1) * 128, tok0:tok0 + L]
                    for i_, base in enumerate((0, DG, 2 * DG, 3 * DG)):
                        S.dma(SP, lambda o=Pb[i_], i=rows(base): nc.sync.dma_start(out=o, in_=i), dP[i_], writes=[tP[i_]])
                    S.op(ACT, lambda: nc.scalar.activation(cv(Pb[4]), pv(Pb[0]), AF.Silu), reads=[tP[0]], writes=[tP[4]])
                    for d in range(2):
                        src, tsrc = (Pb[1], tP[1]) if d == 0 else (Pb[2], tP[2])
                        S.op(ACT, lambda src=src: nc.scalar.activation(cv(Pb[0]), pv(src), AF.Sigmoid), reads=[tsrc], writes=[tP[0]])
                        S.op(DVE, lambda: nc.vector.tensor_scalar(Pb[0], Pb[0], LBc[:, 3, hd:hd + 1], LBc[:, 2, hd:hd + 1],
                                                                  op0=ALU.mult, op1=ALU.add), reads=[tP[0], t_misc], writes=[tP[0]])
                        S.op(ACT, lambda: nc.scalar.activation(Pb[1], Pb[0], AF.Ln), reads=[tP[0]], writes=[tP[1]])
                        S.op(POOL, lambda: nc.gpsimd.tensor_scalar(Pb[0], Pb[0], -1.0, 1.0, op0=ALU.mult, op1=ALU.add),
                             reads=[tP[0]], writes=[tP[0]])
                        S.op(DVE, lambda: nc.vector.tensor_tensor_scan(Pb[5], rmask[:, 0:L], Pb[1], 0.0, op0=ALU.mult, op1=ALU.add),
                             reads=[tP[1], t_misc], writes=[tP[5]])
                        if d == 0:
                            B, tB, O_, tO = Pb[5], tP[5], Pb[1], tP[1]
                            end = 63
                        else:
                            S.op(DVE, lambda: nc.vector.tensor_tensor(Pb[1], Pb[1], Pb[5], op=ALU.subtract),
                                 reads=[tP[1], tP[5]], writes=[tP[1]])
                            S.op(DVE, lambda: nc.vector.tensor_tensor(cv(Pb[1]), cv(Pb[1]), bc(cv(Pb[5])[:, :, 63:64], [128, nch, 64]),
                                                                      op=ALU.add), reads=[tP[1], tP[5]], writes=[tP[1]])
                            B, tB, O_, tO = Pb[1], tP[1], Pb[5], tP[5]
                            end = 0
                        Bc = cv(B)
                        S.op(DVE, lambda Bc=Bc, end=end: nc.vector.tensor_tensor(tmpv[:, 0:nch], Bc[:, :, end], Bc[:, :, 32], op=ALU.subtract),
                             reads=[tB], writes=[t_E[d]])
                        S.op(ACT, lambda: nc.scalar.activation(EE[:, d, 0:nch], tmpv[:, 0:nch], AF.Exp), reads=[t_E[d]], writes=[t_E[d]])
                        S.op(ACT, lambda Bc=Bc: nc.scalar.activation(EM[:, d, 0:nch], Bc[:, :, 32], AF.Exp), reads=[tB], writes=[t_E[d]])
                        S.op(ACT, lambda Bc=Bc, end=end: nc.scalar.activation(ET[:, d, 0:nch], Bc[:, :, end], AF.Exp), reads=[tB], writes=[t_E[d]])
                        S.op(DVE, lambda Bc=Bc, O_=O_: nc.vector.tensor_tensor(cv(O_), Bc, bc(Bc[:, :, 32:33], [128, nch, 64]), op=ALU.subtract),
                             reads=[tB], writes=[tO])
                        S.op(ACT, lambda B=B, O_=O_: nc.scalar.activation(B, O_, AF.Exp, scale=-1.0), reads=[tO], writes=[tB])
                        S.op(POOL, lambda B=B: nc.gpsimd.tensor_tensor(kt[:, d, 0:L], Pb[0], B, op=ALU.mult), reads=[tP[0], tB], writes=[t_kt[d]])
                        S.op(ACT, lambda O_=O_: nc.scalar.activation(O_, O_, AF.Exp), reads=[tO], writes=[tO])
                        S.op(DVE, lambda O_=O_: nc.vector.tensor_tensor(qt[:, d, 0:L], Pb[4], O_, op=ALU.mult), reads=[tP[4], tO], writes=[t_qt[d]])
                    for n0 in range(0, nch, 4):
                        ps_ap, ps_tok = PS.next()
                        for c_ in range(4):
                            transpose_to(ps_ap[0:64, c_ * 128:(c_ + 1) * 128], ps_tok, pv(Pb[3])[:, n0 + c_, :], [tP[3]], ident[:],
                                         signal=(c_ == 3))
                        S.op(ACT, lambda ps_ap=ps_ap, n0=n0: nc.scalar.copy(Vt[:, n0:n0 + 4, :], ps_ap[0:64, :].rearrange("p (c v) -> p c v", v=128)),
                             reads=[ps_tok], writes=[t_Vt])
                    for d in range(2):
                        ktc = cv(kt[:, d, 0:L])
                        for n0 in range(0, nch, 8):
                            nn = min(8, nch - n0)
                            ps_ap, ps_tok = PS.next()
                            pb_ = psb_view(ps_ap)
                            for c_ in range(nn):
                                transpose_to(pb_[0:64, c_ * 128:(c_ + 1) * 128], ps_tok, ktc[:, n0 + c_, :], [t_kt[d]], identb[:],
                                             signal=(c_ == nn - 1))
                            S.op(DVE, lambda pb_=pb_, d=d, n0=n0, nn=nn: nc.vector.tensor_copy(
                                ktT[d][:, n0:n0 + nn, :], pb_[0:64, 0:nn * 128].rearrange("p (c e) -> p c e", e=128)),
                                reads=[ps_tok], writes=[t_ktT[d]])
                    S.op(DVE, lambda: nc.vector.memset(o_acc, 0.0), writes=t_oacc)
                    for d in range(2):
                        if latent:
                            S.dma(SP, lambda d=d: nc.sync.dma_start(out=Sst[:, d, :], in_=st_h_in[d, hd]), d_S[d], writes=[t_S[d]])
                        else:
                            S.op(DVE, lambda d=d: nc.vector.memset(Sst[:, d, :], 0.0), writes=[t_S[d]])
                    for step in range(nch):
                        for d in range(2):
                            n = step if d == 0 else nch - 1 - step
                            qtc = cv(qt[:, d, 0:L])
                            ktc = cv(kt[:, d, 0:L])
                            smb, tsm = SMR.next()
                            S.op(DVE, lambda smb=smb, d=d, n=n: nc.vector.tensor_scalar(smb, Sst[:, d, :], EM[:, d, n:n + 1], None, op0=ALU.mult),
                                 reads=[t_S[d], t_E[d]], writes=[tsm])
                            pa, tpa = PS.next()
                            mm_group(pa[0:64, 0:64], tpa, [(ktc[:, n, :], qtc[:, n, :], [t_kt[d], t_qt[d]])])
                            atb, tat = ATR.next()
                            msk = mU_i if d == 0 else mL_i
                            S.op(DVE, lambda atb=atb, pa=pa, msk=msk: nc.vector.tensor_tensor(atb, pa[0:64, 0:64], msk[0:64, 0:64], op=ALU.mult),
                                 reads=[tpa, t_const], writes=[tat])
                            po, tpo = PS.next()
                            mm_group(po[0:64, 0:128], tpo, [(qtc[:, n, :], smb, [t_qt[d], tsm]),
                                                            (atb, Vt[:, n, :], [tat, t_Vt])])
                            S.op(DVE, lambda po=po, n=n: nc.vector.tensor_tensor(o_acc[:, n, :], o_acc[:, n, :], po[0:64, 0:128], op=ALU.add),
                                 reads=[tpo] + t_oacc, writes=t_oacc)
                            pst_, tps = PS.next()
                            mm_group(pst_[:, 0:128], tps, [(ktT[d][:, n, :], Vt[:, n, :], [t_ktT[d], t_Vt])])
                            stt_, tst = STR.next()
                            S.op(ACT, lambda stt_=stt_, pst_=pst_, d=d, n=n: nc.scalar.activation(stt_, pst_[:, 0:128], AF.Copy, scale=EE[:, d, n:n + 1]),
                                 reads=[tps, t_E[d]], writes=[tst])
                            S.op(DVE, lambda stt_=stt_, d=d, n=n: nc.vector.scalar_tensor_tensor(
                                Sst[:, d, :], Sst[:, d, :], ET[:, d, n:n + 1], stt_, op0=ALU.mult, op1=ALU.add),
                                reads=[t_S[d], tst, t_E[d]], writes=[t_S[d]])
                    S.dma(SP, lambda: nc.sync.dma_start(out=Pb[4], in_=rows(4 * DG)), dP[4], writes=[tP[4]])
                    S.op(ACT, lambda: nc.scalar.activation(Pb[4], Pb[4], AF.Silu), reads=[tP[4]], writes=[tP[4]])
                    S.op(DVE, lambda: nc.vector.tensor_tensor(onb, o_acc, o_acc, op=ALU.mult), reads=t_oacc, writes=t_onb)
                    S.op(DVE, lambda: nc.vector.tensor_reduce(ssb[:, 0:nch], onb, axis=AX.X, op=ALU.add), reads=t_onb, writes=[t_ss])
                    S.op(ACT, lambda: nc.scalar.activation(ssb[:, 0:nch], ssb[:, 0:nch], AF.Sqrt, bias=epsv[0:64, :], scale=1.0 / 128),
                         reads=[t_ss, t_const], writes=[t_ss])
                    S.op(DVE, lambda: nc.vector.reciprocal(ssb[:, 0:nch], ssb[:, 0:nch]), reads=[t_ss], writes=[t_ss])
                    S.op(DVE, lambda: nc.vector.tensor_tensor(onb, o_acc, bc(ssb[:, 0:nch].rearrange("p (n o) -> p n o", o=1), [64, nch, 128]),
                                                              op=ALU.mult), reads=t_oacc + [t_ss], writes=t_onb)
                    for n0 in range(0, nch, 4):
                        ps_ap, ps_tok = PS.next()
                        for c_ in range(4):
                            transpose_to(ps_ap[:, c_ * 64:(c_ + 1) * 64], ps_tok, onb[:, n0 + c_, :], t_onb, ident[0:64, 0:64],
                                         signal=(c_ == 3))
                        S.op(DVE, lambda ps_ap=ps_ap, n0=n0: nc.vector.scalar_tensor_tensor(
                            pv(yT[:, 0:L])[:, n0:n0 + 4, :], ps_ap[:, 0:256].rearrange("p (c t) -> p c t", t=64), gainv[:, 0:1],
                            pv(Pb[4])[:, n0:n0 + 4, :], op0=ALU.mult, op1=ALU.mult),
                            reads=[ps_tok, tP[4], t_misc], writes=[t_yT])
                    S.dma(SP, lambda: nc.sync.dma_start(out=yhT[hd * 128:(hd + 1) * 128, tok0:tok0 + L], in_=yT[:, 0:L]), d_yT, reads=[t_yT])
                    if not latent:
                        for d in range(2):
                            S.dma(SP, lambda d=d: nc.sync.dma_start(out=nsh_out[sidx, d, hd], in_=Sst[:, d, :]), d_S[d], reads=[t_S[d]])

            seq(0, 256, False, 0)
            seq(256, 256, False, 1)
            seq(512, 4096, True, 2)
            S.barrier()


        def phaseB_rwkv():
            AR.reset()
            CK = 128
            CDEC = math.exp(-0.5)
            NRC = (cfg.RW + 127) // 128
            RLAST = cfg.RW - (NRC - 1) * 128
            ZTN = NRC * 130 + NRC * 128
            assert ZTN >= 7 * NP * 128
            ZT = AR.alloc([128, ZTN], F32)
            Z = ZT[:, 0:NRC * 130].rearrange("p (k t) -> p k t", t=130)
            TMPs = ZT[:, NRC * 130:ZTN].rearrange("p (k t) -> p k t", t=128)
            Dn = [None] + [ZT[:, i * NP * 128:(i + 1) * NP * 128].rearrange("p (j t) -> p j t", t=128) for i in range(7)]
            tD = [None] + [Tok(f"D{i}") for i in range(1, 8)]
            t_Z, t_TMP, t_ZS = Tok("Z"), Tok("TMPs"), Tok("ZS")
            d_Z = S.newdsem("d_Z")
            ZS = AR.alloc([128, NRC, 128], F32)
            CR = AR.alloc([128, NP, 2, 128], F32)
            BT = AR.alloc([128, NP, 128], F32)
            KT = AR.alloc([128, NP, 128], F32)
            t_CR, t_BT, t_KT = Tok("CR"), Tok("BT"), Tok("KT")
            Sst = AR.alloc([128, NP, 64], F32)
            t_S = [Tok(f"rS{j}") for j in range(NP)]
            d_S = S.newdsem("d_rS")
            RAW = AR.alloc([128, NP, 64], F32)
            t_RAW = Tok("RAW")
            GC = AR.alloc([128, NP], F32)
            TOTv = AR.alloc([128, NP], F32)
            t_GC = Tok("GC")
            w2b = AR.alloc([128, 2, DR], BF16)
            a2b = AR.alloc([128, 2, DR], BF16)
            g2b = AR.alloc([128, 4, DR], BF16)
            d_lw = S.newdsem("d_lw")
            t_lw = Tok("lw")
            MUP = AR.alloc([128, NRC], F32)
            MUN = AR.alloc([128, NRC], F32)
            W0c = AR.alloc([128, 2, NP], F32)
            A0c = AR.alloc([128, 2, NP], F32)
            KKc = AR.alloc([128, NP], F32)
            KAc = AR.alloc([128, NP], F32)
            KA1c = AR.alloc([128, NP], F32)
            RKc = AR.alloc([128, NP], F32)
            LNWc = AR.alloc([128, NP], F32)
            LNBc = AR.alloc([128, NP], F32)
            BO = AR.alloc([128, 128], F32)
            rmask2 = AR.alloc([128, NP * 128], BF16)
            maskA = AR.alloc([128, 2, 2, 128], F32)
            maskM = AR.alloc([128, 2, 128], F32)
            ident2 = AR.alloc([128, 2, 128], F32)
            tinyv = AR.alloc([128, 1], F32)
            gnepsv = AR.alloc([128, 1], F32)
            TWb = AR.alloc([128, 128], BF16)
            ADb = AR.alloc([128, 128], BF16)
            SGD = AR.alloc([128, 4, 128], BF16)
            t_TW, t_AD, t_SGD = Tok("TW"), Tok("AD"), Tok("SGD")
            t_ofT = Tok("ofT")
            d_of = S.newdsem("d_of")
            d_yr = S.newdsem("d_yr")
            G = 2
            slots = []
            for g_ in range(G):
                sl = dict(A1s=AR.alloc([128, 2, 2, 128], F32), A2s=AR.alloc([128, 2, 2, 128], F32), M0s=AR.alloc([128, 2, 128], F32),
                          NM=[AR.alloc([128, 2, 2, 128], F32), AR.alloc([128, 2, 2, 128], F32)], P=AR.alloc([128, 2, 128], F32),
                          Xs=AR.alloc([128, 2, 64], F32), Us=AR.alloc([128, 2, 64], F32), TT=AR.alloc([128, 3, 128], F32),
                          OS=AR.alloc([128, 128], F32), T1=AR.alloc([128, 128], F32), T2=AR.alloc([128, 128], F32))
                sl["tok"] = {k: Tok(f"{k}{g_}") for k in ("A1s", "A2s", "M0s", "NM0", "NM1", "P", "Xs", "Us", "TT", "OS", "T1", "T2")}
                slots.append(sl)

            load_vec_fm(MUP[:], mu_prev, cfg.RW)
            load_vec_fm(MUN[:], mu_next, cfg.RW)
            for d in range(2):
                load_vec_fm(W0c[:, d, :], rw_w0[d, :], DR)
                load_vec_fm(A0c[:, d, :], rw_a0[d, :], DR)
            load_vec_fm(KKc[:], rw_kk, DR)
            load_vec_fm(KAc[:], rw_ka, DR)
            load_vec_fm(RKc[:], rw_rk, DR)
            load_vec_fm(LNWc[:], rw_lnw, DR)
            load_vec_fm(LNBc[:], rw_lnb, DR)
            S.op(DVE, lambda: nc.vector.tensor_scalar(KA1c[:], KAc[:], -1.0, 1.0, op0=ALU.mult, op1=ALU.add), reads=[t_misc], writes=[t_misc])
            for d in range(2):
                S.dma(POOL, lambda d=d: nc.gpsimd.dma_start(out=w2b[:, d, :], in_=rw_w2[d]), d_lw, writes=[t_lw])
                S.dma(POOL, lambda d=d: nc.gpsimd.dma_start(out=a2b[:, d, :], in_=rw_a2[d]), d_lw, writes=[t_lw])
            for c_ in range(4):
                nr = min(128, 480 - c_ * 128)
                S.dma(POOL, lambda c_=c_, nr=nr: nc.gpsimd.dma_start(out=g2b[0:nr, c_, :], in_=rw_g2[c_ * 128:c_ * 128 + nr, :]), d_lw, writes=[t_lw])
            S.op(DVE, lambda: nc.vector.memset(BO[:], 0.0), writes=[t_misc])
            S.op(DVE, lambda: nc.vector.memset(BO[0:64, 0:64], 1.0), writes=[t_misc])
            S.op(DVE, lambda: nc.vector.memset(BO[64:128, 64:128], 1.0), writes=[t_misc])
            S.op(DVE, lambda: nc.vector.memset(rmask2[:], 1.0), writes=[t_misc])
            S.op(DVE, lambda: nc.vector.memset(rmask2[:].rearrange("p (n i) -> p n i", i=128)[:, :, 0:1], 0.0), writes=[t_misc])
            for dd, (ms, mi, mm_) in enumerate(((mU_s, mU_i, mL_s), (mL_s, mL_i, mU_s))):
                S.op(DVE, lambda dd=dd, ms=ms: nc.vector.tensor_copy(maskA[:, dd, 0, :], ms[:]), reads=[t_const], writes=[t_misc])
                S.op(DVE, lambda dd=dd, mi=mi: nc.vector.tensor_copy(maskA[:, dd, 1, :], mi[:]), reads=[t_const], writes=[t_misc])
                S.op(DVE, lambda dd=dd, mm_=mm_: nc.vector.tensor_copy(maskM[:, dd, :], mm_[:]), reads=[t_const], writes=[t_misc])
            for h in range(2):
                S.op(DVE, lambda h=h: nc.vector.tensor_copy(ident2[:, h, :], ident[:]), reads=[t_const], writes=[t_misc])
            S.op(DVE, lambda: nc.vector.memset(tinyv[:], 1e-24), writes=[t_misc])
            S.op(DVE, lambda: nc.vector.memset(gnepsv[:], GN_EPS), writes=[t_misc])
            S.op(DVE, lambda: nc.vector.memset(ZT[:], 0.0), writes=[t_Z, t_TMP] + tD[1:])

            i_wd = [3 * NP, 3 * NP + 1]
            i_ad = [3 * NP + 2, 3 * NP + 3]
            i_gd = 3 * NP + 4
            rv = ZS[:, 0:NP, :]
            kv = ZS[:, NP:2 * NP, :]
            vv = ZS[:, 2 * NP:3 * NP, :]
            bN = lambda ap: bc(ap.rearrange("p (j o) -> p j o", o=1), [128, NP, 128])

            def lora(dst, tdst, wb, d, srcb, tsrc, biasc):
                for j0 in range(0, NP, 4):
                    nj = min(4, NP - j0)
                    pw, tpw = PS.next()
                    for jj in range(nj):
                        j = j0 + jj
                        mm_group(pw[:, jj * 128:(jj + 1) * 128], tpw, [(wb[:, d, j * 128:(j + 1) * 128], srcb[:], [t_lw, tsrc])])
                    for jj in range(nj):
                        j = j0 + jj
                        S.op(ACT, lambda pw=pw, jj=jj, j=j: nc.scalar.activation(dst[:, j, :], pw[:, jj * 128:(jj + 1) * 128], AF.Sigmoid,
                                                                                bias=biasc[:, d, j:j + 1], scale=1.0),
                             reads=[tpw, t_misc], writes=[tdst])

            def chunk_pass(tok0, L, d, ck, first, last):
                t0 = tok0 + ck * CK
                c_lo, t_lo = (1, t0) if first else (0, t0 - 1)
                c_hi, t_hi = (129, t0 + CK) if last else (130, t0 + CK + 1)
                allD = tD[1:]
                if NRC > 1:
                    S.dma(SP, lambda: nc.sync.dma_start(out=Z[:, 0:NRC - 1, c_lo:c_hi],
                                                        in_=zTr[0:(NRC - 1) * 128, t_lo:t_hi].rearrange("(k p) t -> p k t", p=128)),
                          d_Z, writes=[t_Z] + allD)
                S.dma(SP, lambda: nc.sync.dma_start(out=Z[0:RLAST, NRC - 1, c_lo:c_hi], in_=zTr[(NRC - 1) * 128:cfg.RW, t_lo:t_hi]),
                      d_Z, writes=[t_Z] + allD)
                if first:
                    S.op(DVE, lambda: nc.vector.memset(Z[:, :, 0:1], 0.0), writes=[t_Z])
                if last:
                    S.op(DVE, lambda: nc.vector.memset(Z[:, :, 129:130], 0.0), writes=[t_Z])
                bM = lambda m: bc(m.rearrange("p (k o) -> p k o", o=1), [128, NRC, 128])
                S.op(DVE, lambda: nc.vector.tensor_tensor(TMPs, Z[:, :, 0:128], Z[:, :, 1:129], op=ALU.subtract), reads=[t_Z], writes=[t_TMP])
                S.op(POOL, lambda: nc.gpsimd.tensor_tensor(TMPs, TMPs, bM(MUP[:]), op=ALU.mult), reads=[t_misc], writes=[t_TMP])
                S.op(DVE, lambda: nc.vector.tensor_tensor(ZS[:], TMPs, Z[:, :, 1:129], op=ALU.add), reads=[t_TMP, t_Z], writes=[t_ZS])
                S.op(POOL, lambda: nc.gpsimd.tensor_tensor(TMPs, Z[:, :, 2:130], Z[:, :, 1:129], op=ALU.subtract), reads=[t_Z], writes=[t_TMP])
                S.op(POOL, lambda: nc.gpsimd.tensor_tensor(TMPs, TMPs, bM(MUN[:]), op=ALU.mult), reads=[t_misc], writes=[t_TMP])
                S.op(DVE, lambda: nc.vector.tensor_tensor(ZS[:], ZS[:], TMPs, op=ALU.add), reads=[t_TMP], writes=[t_ZS])
                D1, D2, D3, D4, D5, D6, D7 = Dn[1:]
                f2 = lambda ap: ap.rearrange("p j t -> p (j t)")
                S.op(ACT, lambda: nc.scalar.activation(TWb[:], ZS[:, i_wd[d], :], AF.Tanh), reads=[t_ZS], writes=[t_TW])
                lora(D1, tD[1], w2b, d, TWb, t_TW, W0c)
                S.op(DVE, lambda: nc.vector.tensor_tensor_scan(f2(D2), rmask2[:], f2(D1), 0.0, op0=ALU.mult, op1=ALU.add),
                     reads=[tD[1], t_misc], writes=[tD[2]])
                S.op(DVE, lambda: nc.vector.tensor_copy(TOTv[:], D2[:, :, 127]), reads=[tD[2]], writes=[t_GC])
                if d == 1:
                    S.op(DVE, lambda: nc.vector.tensor_tensor(D2, D1, D2, op=ALU.subtract), reads=[tD[1]], writes=[tD[2]])
                    S.op(DVE, lambda: nc.vector.tensor_tensor(D2, D2, bN(TOTv[:]), op=ALU.add), reads=[t_GC], writes=[tD[2]])
                S.op(ACT, lambda: nc.scalar.activation(GC[:], TOTv[:], AF.Exp, scale=-CDEC), reads=[t_GC], writes=[t_GC])
                S.op(ACT, lambda: nc.scalar.activation(D3, D2, AF.Exp, scale=-CDEC), reads=[tD[2]], writes=[tD[3]])
                S.op(POOL, lambda: nc.gpsimd.tensor_tensor(CR[:, :, 1, :], rv, D3, op=ALU.mult), reads=[t_ZS, tD[3]], writes=[t_CR])
                S.op(DVE, lambda: nc.vector.tensor_tensor(D1, D2, D1, op=ALU.subtract), reads=[tD[2]], writes=[tD[1]])
                S.op(ACT, lambda: nc.scalar.activation(D1, D1, AF.Exp, scale=-CDEC), reads=[], writes=[tD[1]])
                S.op(ACT, lambda: nc.scalar.activation(D2, D2, AF.Exp, scale=CDEC), reads=[], writes=[tD[2]])
                S.op(POOL, lambda: nc.gpsimd.tensor_tensor(D3, kv, bN(KKc[:]), op=ALU.mult), reads=[t_ZS, t_misc], writes=[tD[3]])
                S.op(ACT, lambda: nc.scalar.activation(D4, D3, AF.Square), reads=[tD[3]], writes=[tD[4]])
                for j0 in range(0, NP, 4):
                    nj = min(4, NP - j0)
                    pk, tpk = PS.next()
                    mm_group(pk[:, 0:nj * 128], tpk, [(BO[:], f2(D4[:, j0:j0 + nj, :]), [t_misc, tD[4]])])
                    S.op(ACT, lambda pk=pk, j0=j0, nj=nj: nc.scalar.activation(f2(D4[:, j0:j0 + nj, :]), pk[:, 0:nj * 128], AF.Sqrt,
                                                                              bias=tinyv[:], scale=1.0), reads=[tpk, t_misc], writes=[tD[4]])
                S.op(DVE, lambda: nc.vector.reciprocal(D4, D4), reads=[], writes=[tD[4]])
                S.op(DVE, lambda: nc.vector.tensor_tensor(D3, D3, D4, op=ALU.mult), reads=[tD[4]], writes=[tD[3]])
                S.op(POOL, lambda: nc.gpsimd.tensor_tensor(CR[:, :, 0, :], D3, D1, op=ALU.mult), reads=[tD[3], tD[1]], writes=[t_CR])
                S.op(ACT, lambda: nc.scalar.copy(ADb[:], ZS[:, i_ad[d], :]), reads=[t_ZS], writes=[t_AD])
                lora(D5, tD[5], a2b, d, ADb, t_AD, A0c)
                S.op(DVE, lambda: nc.vector.tensor_tensor(D6, D3, D5, op=ALU.mult), reads=[tD[3], tD[5]], writes=[tD[6]])
                S.op(POOL, lambda: nc.gpsimd.tensor_tensor(BT[:], D6, D2, op=ALU.mult), reads=[tD[6], tD[2]], writes=[t_BT])
                S.op(DVE, lambda: nc.vector.tensor_tensor(D6, D5, bN(KAc[:]), op=ALU.mult), reads=[tD[5], t_misc], writes=[tD[6]])
                S.op(DVE, lambda: nc.vector.tensor_tensor(D6, D6, bN(KA1c[:]), op=ALU.add), reads=[t_misc], writes=[tD[6]])
                S.op(POOL, lambda: nc.gpsimd.tensor_tensor(D6, D6, kv, op=ALU.mult), reads=[t_ZS], writes=[tD[6]])
                S.op(DVE, lambda: nc.vector.tensor_tensor(KT[:], D6, D2, op=ALU.mult), reads=[tD[6], tD[2]], writes=[t_KT])
                if d == 1:
                    S.op(ACT, lambda: nc.scalar.copy(ADb[:], ZS[:, i_ad[0], :]), reads=[t_ZS], writes=[t_AD])
                    lora(D5, tD[5], a2b, 0, ADb, t_AD, A0c)
                    S.op(DVE, lambda: nc.vector.tensor_tensor(D5, D5, bN(KAc[:]), op=ALU.mult), reads=[t_misc], writes=[tD[5]])
                    S.op(DVE, lambda: nc.vector.tensor_tensor(D5, D5, bN(KA1c[:]), op=ALU.add), reads=[t_misc], writes=[tD[5]])
                    S.op(POOL, lambda: nc.gpsimd.tensor_tensor(D5, D5, kv, op=ALU.mult), reads=[t_ZS], writes=[tD[5]])
                    S.op(DVE, lambda: nc.vector.tensor_tensor(D5, D5, D6, op=ALU.add), reads=[tD[6]], writes=[tD[5]])
                    S.op(POOL, lambda: nc.gpsimd.tensor_tensor(D5, D5, rv, op=ALU.mult), reads=[t_ZS], writes=[tD[5]])
                    S.op(DVE, lambda: nc.vector.tensor_tensor(D7, D5, bN(RKc[:]), op=ALU.mult), reads=[tD[5], t_misc], writes=[tD[7]])
                    for c_ in range(4):
                        nr = min(128, 480 - c_ * 128)
                        S.op(ACT, lambda c_=c_, nr=nr: nc.scalar.activation(SGD[0:nr, c_, :], ZS[0:nr, i_gd + c_, :], AF.Sigmoid),
                             reads=[t_ZS], writes=[t_SGD])
                    S.dma(SP, lambda: nc.sync.dma_start(out=D1, in_=ofT[:, t0:t0 + CK].rearrange("(j p) t -> p j t", p=128)), d_of,
                          reads=[t_ofT], writes=[tD[1]])
                OB = D1 if d == 0 else None
                YT = f2(D3).bitcast(BF16)[:, 0:NP * 128].rearrange("p (j t) -> p j t", t=128)

                def pair(j, sl):
                    tk = sl["tok"]
                    A1s, A2s, M0s, NM, Pm, Xs, Us, TT = sl["A1s"], sl["A2s"], sl["M0s"], sl["NM"], sl["P"], sl["Xs"], sl["Us"], sl["TT"]
                    hp = [slice(0, 64), slice(64, 128)]
                    pt, tpt = PS.next()
                    for i_, (src, tsrc) in enumerate(((BT[:, j, :], t_BT), (KT[:, j, :], t_KT), (vv[:, j, :], t_ZS))):
                        transpose_to(pt[:, i_ * 128:(i_ + 1) * 128], tpt, src, [tsrc], ident[:], signal=(i_ == 2))
                    S.op(ACT, lambda: nc.scalar.copy(TT[:].rearrange("p a t -> p (a t)"), pt[:, 0:384]), reads=[tpt], writes=[tk["TT"]])
                    p1, tp1 = PS.next()
                    p2, tp2 = PS.next()
                    p3, tp3 = PS.next()
                    for h in range(2):
                        crh = CR[hp[h], j, :, :].rearrange("p a t -> p (a t)")
                        mm_group(p1[:, h * 256:(h + 1) * 256], tp1, [(BT[hp[h], j, :], crh, [t_BT, t_CR])])
                        mm_group(p2[:, h * 256:(h + 1) * 256], tp2, [(KT[hp[h], j, :], crh, [t_KT, t_CR])])
                        mm_group(p3[:, h * 128:(h + 1) * 128], tp3, [(CR[hp[h], j, 0, :], BT[hp[h], j, :], [t_BT, t_CR])])
                    mA = bc(maskA[:, d, :, :].rearrange("p (o a) t -> p o a t", o=1), [128, 2, 2, 128])
                    S.op(DVE, lambda: nc.vector.tensor_tensor(A1s[:], p1[:].rearrange("p (h a t) -> p h a t", h=2, a=2), mA, op=ALU.mult),
                         reads=[tp1, t_misc], writes=[tk["A1s"]])
                    S.op(DVE, lambda: nc.vector.tensor_tensor(A2s[:], p2[:].rearrange("p (h a t) -> p h a t", h=2, a=2), mA, op=ALU.mult),
                         reads=[tp2, t_misc], writes=[tk["A2s"]])
                    mM = bc(maskM[:, d, :].rearrange("p (o t) -> p o t", o=1), [128, 2, 128])
                    S.op(DVE, lambda: nc.vector.tensor_tensor(M0s[:], p3[:, 0:256].rearrange("p (h t) -> p h t", h=2), mM, op=ALU.mult),
                         reads=[tp3, t_misc], writes=[tk["M0s"]])
                    S.op(DVE, lambda: nc.vector.scalar_tensor_tensor(Pm[:], A1s[:, :, 0, :], -1.0, ident2[:], op0=ALU.mult, op1=ALU.add),
                         reads=[tk["A1s"], t_misc], writes=[tk["P"]])
                    yield
                    Nprev = lambda h: A1s[:, h, 0, :]
                    Mprev = lambda h: M0s[:, h, :]
                    tNp, tMp = tk["A1s"], tk["M0s"]
                    for lev in range(1, 7):
                        lastl = (lev == 6)
                        nm = NM[lev % 2]
                        tnm = tk[f"NM{lev % 2}"]
                        pq, tpq = PS.next()
                        for h in range(2):
                            if not lastl:
                                mm_group(pq[:, h * 128:(h + 1) * 128], tpq, [(Mprev(h), Nprev(h), [tNp, tMp])])
                            mm_group(pq[:, 256 + h * 128:256 + (h + 1) * 128], tpq, [(Nprev(h), Mprev(h), [tNp, tMp])])
                        if lastl:
                            S.op(ACT, lambda nm=nm, pq=pq: nc.scalar.copy(nm[:, 1, :, :].rearrange("p h t -> p (h t)"), pq[:, 256:512]),
                                 reads=[tpq], writes=[tnm])
                        else:
                            S.op(ACT, lambda nm=nm, pq=pq: nc.scalar.copy(nm[:].rearrange("p a h t -> p (a h t)"), pq[:]),
                                 reads=[tpq], writes=[tnm])
                        yield
                        pp, tpp = PS.next()
                        for h in range(2):
                            mm_group(pp[:, h * 128:(h + 1) * 128], tpp, [(nm[:, 1, h, :], Pm[:, h, :], [tnm, tk["P"]])])
                        S.op(DVE, lambda pp=pp: nc.vector.tensor_tensor(Pm[:].rearrange("p h t -> p (h t)"), Pm[:].rearrange("p h t -> p (h t)"),
                                                                        pp[:, 0:256], op=ALU.add), reads=[tpp], writes=[tk["P"]])
                        Nprev = lambda h, nm=nm: nm[:, 0, h, :]
                        Mprev = lambda h, nm=nm: nm[:, 1, h, :]
                        tNp = tMp = tnm
                        yield
                    px, tpx = PS.next()
                    for h in range(2):
                        mm_group(px[:, h * 64:(h + 1) * 64], tpx, [(CR[hp[h], j, 0, :], Sst[hp[h], j, :], [t_CR, t_S[j]]),
                                                                  (A2s[:, h, 0, :], TT[:, 2, h * 64:(h + 1) * 64], [tk["A2s"], tk["TT"]])])
                    S.op(ACT, lambda: nc.scalar.activation(Xs[:].rearrange("p h v -> p (h v)"), px[:, 0:128], AF.Copy, scale=-1.0),
                         reads=[tpx], writes=[tk["Xs"]])
                    yield
                    pu, tpu = PS.next()
                    for h in range(2):
                        mm_group(pu[:, h * 64:(h + 1) * 64], tpu, [(Pm[:, h, :], Xs[:, h, :], [tk["P"], tk["Xs"]])])
                    S.op(ACT, lambda: nc.scalar.copy(Us[:].rearrange("p h v -> p (h v)"), pu[:, 0:128]), reads=[tpu], writes=[tk["Us"]])
                    yield
                    po, tpo = PS.next()
                    for h in range(2):
                        mm_group(po[hp[h], 0:128], tpo, [(Sst[hp[h], j, :], CR[hp[h], j, 1, :], [t_S[j], t_CR]),
                                                        (Us[:, h, :], A1s[:, h, 1, :], [tk["Us"], tk["A1s"]]),
                                                        (TT[:, 2, h * 64:(h + 1) * 64], A2s[:, h, 1, :], [tk["TT"], tk["A2s"]])])
                    pS_, tpS = PS.next()
                    for h in range(2):
                        mm_group(pS_[hp[h], 0:64], tpS, [(TT[:, 0, h * 64:(h + 1) * 64], Us[:, h, :], [tk["TT"], tk["Us"]]),
                                                        (TT[:, 1, h * 64:(h + 1) * 64], TT[:, 2, h * 64:(h + 1) * 64], [tk["TT"]])])
                    S.op(DVE, lambda: nc.vector.tensor_tensor(Sst[:, j, :], Sst[:, j, :], pS_[:, 0:64], op=ALU.add), reads=[tpS], writes=[t_S[j]])
                    S.op(DVE, lambda: nc.vector.tensor_scalar(Sst[:, j, :], Sst[:, j, :], GC[:, j:j + 1], None, op0=ALU.mult),
                         reads=[t_GC], writes=[t_S[j]])
                    if d == 0:
                        S.op(ACT, lambda: nc.scalar.copy(D1[:, j, :], po[:, 0:128]), reads=[tpo], writes=[tD[1]])
                        return
                    yield
                    OS, T1, T2 = sl["OS"], sl["T1"], sl["T2"]
                    S.op(DVE, lambda: nc.vector.tensor_tensor(OS[:], po[:, 0:128], D1[:, j, :], op=ALU.add), reads=[tpo, tD[1]], writes=[tk["OS"]])
                    S.op(ACT, lambda: nc.scalar.activation(T1[:], OS[:], AF.Square), reads=[tk["OS"]], writes=[tk["T1"]])
                    pm, tpm = PS.next()
                    mm_group(pm[:, 0:128], tpm, [(BO[:], OS[:], [t_misc, tk["OS"]])])
                    mm_group(pm[:, 128:256], tpm, [(BO[:], T1[:], [t_misc, tk["T1"]])])
                    mm_group(pm[:, 256:384], tpm, [(BO[:], D7[:, j, :], [t_misc, tD[7]])])
                    items = []
                    for c_ in range(4):
                        nr = min(128, 480 - c_ * 128)
                        items.append((g2b[0:nr, c_, j * 128:(j + 1) * 128], SGD[0:nr, c_, :], [t_lw, t_SGD]))
                    pg, tpg = PS.next()
                    mm_group(pg[:, 0:128], tpg, items)
                    yield
                    S.op(ACT, lambda: nc.scalar.activation(T2[:], pm[:, 0:128], AF.Copy, scale=1.0 / 64), reads=[tpm], writes=[tk["T2"]])
                    S.op(DVE, lambda: nc.vector.tensor_tensor(T1[:], T2[:], T2[:], op=ALU.mult), reads=[tk["T2"]], writes=[tk["T1"]])
                    S.op(DVE, lambda: nc.vector.scalar_tensor_tensor(T1[:], pm[:, 128:256], 1.0 / 64, T1[:], op0=ALU.mult, op1=ALU.subtract),
                         reads=[tpm], writes=[tk["T1"]])
                    S.op(ACT, lambda: nc.scalar.activation(T1[:], T1[:], AF.Sqrt, bias=gnepsv[:], scale=1.0), reads=[t_misc], writes=[tk["T1"]])
                    S.op(DVE, lambda: nc.vector.reciprocal(T1[:], T1[:]), reads=[], writes=[tk["T1"]])
                    S.op(DVE, lambda: nc.vector.tensor_tensor(OS[:], OS[:], T2[:], op=ALU.subtract), reads=[tk["T2"]], writes=[tk["OS"]])
                    S.op(DVE, lambda: nc.vector.tensor_tensor(OS[:], OS[:], T1[:], op=ALU.mult), reads=[tk["T1"]], writes=[tk["OS"]])
                    S.op(DVE, lambda: nc.vector.tensor_scalar(OS[:], OS[:], LNWc[:, j:j + 1], LNBc[:, j:j + 1], op0=ALU.mult, op1=ALU.add),
                         reads=[t_misc], writes=[tk["OS"]])
                    S.op(DVE, lambda: nc.vector.tensor_tensor(T2[:], pm[:, 256:384], vv[:, j, :], op=ALU.mult), reads=[tpm, t_ZS], writes=[tk["T2"]])
                    S.op(DVE, lambda: nc.vector.tensor_tensor(OS[:], OS[:], T2[:], op=ALU.add), reads=[tk["T2"]], writes=[tk["OS"]])
                    S.op(DVE, lambda: nc.vector.tensor_tensor(YT[:, j, :], OS[:], pg[:, 0:128], op=ALU.mult), reads=[tk["OS"], tpg], writes=[tD[3]])

                for j0 in range(0, NP, G):
                    gens = [pair(j, slots[(j - j0) % G]) for j in range(j0, min(NP, j0 + G))]
                    while gens:
                        nxt = []
                        for g_ in gens:
                            try:
                                next(g_)
                                nxt.append(g_)
                            except StopIteration:
                                pass
                        gens = nxt
                if d == 0:
                    S.dma(SP, lambda: nc.sync.dma_start(out=ofT[:, t0:t0 + CK].rearrange("(j p) t -> p j t", p=128), in_=D1), d_of,
                          reads=[tD[1]], writes=[t_ofT])
                else:
                    S.dma(SP, lambda: nc.sync.dma_start(out=yrT[:, t0:t0 + CK].rearrange("(j p) t -> p j t", p=128), in_=YT), d_yr,
                          reads=[tD[3]])

            def seq(tok0, L, latent, sidx):
                nck = L // CK
                hp = [slice(0, 64), slice(64, 128)]
                for d in range(2):
                    if latent:
                        S.dma(SP, lambda d=d: nc.sync.dma_start(out=RAW[:], in_=st_r_in[d].rearrange("(j h) v k -> (h v) j k", h=2)), d_S,
                              writes=[t_RAW])
                        for j0 in range(0, NP, 8):
                            nj = min(8, NP - j0)
                            ps_ap, ps_tok = PS.next()
                            for jj in range(nj):
                                for h in range(2):
                                    mm_group(ps_ap[hp[h], jj * 64:(jj + 1) * 64], ps_tok,
                                             [(RAW[hp[h], j0 + jj, :], ident[hp[h], hp[h]], [t_RAW, t_const])])
                            S.op(DVE, lambda ps_ap=ps_ap, j0=j0, nj=nj: nc.vector.tensor_copy(Sst[:, j0:j0 + nj, :].rearrange("p j v -> p (j v)"),
                                                                                             ps_ap[:, 0:nj * 64]),
                                 reads=[ps_tok], writes=t_S[j0:j0 + nj])
                    else:
                        S.op(DVE, lambda: nc.vector.memset(Sst[:], 0.0), writes=t_S)
                    for ci in range(nck):
                        ck = ci if d == 0 else nck - 1 - ci
                        chunk_pass(tok0, L, d, ck, ck == 0, ck == nck - 1)
                    if not latent:
                        for j0 in range(0, NP, 8):
                            nj = min(8, NP - j0)
                            ps_ap, ps_tok = PS.next()
                            for jj in range(nj):
                                for h in range(2):
                                    mm_group(ps_ap[hp[h], jj * 64:(jj + 1) * 64], ps_tok,
                                             [(Sst[hp[h], j0 + jj, :], ident[hp[h], hp[h]], [t_S[j0 + jj], t_const])])
                            S.op(DVE, lambda ps_ap=ps_ap, j0=j0, nj=nj: nc.vector.tensor_copy(RAW[:, j0:j0 + nj, :].rearrange("p j v -> p (j v)"),
                                                                                             ps_ap[:, 0:nj * 64]),
                                 reads=[ps_tok], writes=[t_RAW])
                        S.dma(SP, lambda d=d: nc.sync.dma_start(out=nsr_out[sidx, d].rearrange("(j h) v k -> (h v) j k", h=2), in_=RAW[:]), d_S,
                              reads=[t_RAW])

            seq(0, 256, False, 0)
            seq(256, 256, False, 1)
            seq(512, 4096, True, 2)
            S.barrier()

        phase0()
        if "A" in cfg.phases:
            for tt_i in range(cfg.NT):
                phaseA(tt_i)
        S.barrier()
        if "B" in cfg.phases:
            phaseB_hgrn()
            if getattr(cfg, "test_mode", "") != "hgrn":
                phaseB_rwkv()
        if "C" in cfg.phases:
            for tt_i in range(cfg.NT):
                phaseC(tt_i)
        S.barrier()
    return nc


_WEIGHT_KEYS = ["w_mod", "b_mod", "norm_ffn1", "norm_mix", "norm_ffn2", "ffn1_w_in", "ffn1_w_out", "ffn2_w_in",
                "ffn2_w_out", "w_in", "rwkv_mu_prev", "rwkv_mu_next", "rwkv_w0", "rwkv_w2", "rwkv_a0", "rwkv_a2",
                "rwkv_g2", "rwkv_k_k", "rwkv_k_a", "rwkv_r_k", "rwkv_ln_w", "rwkv_ln_b", "w_branch_rwkv",
                "w_branch_hgrn", "w_out"]


def make_in_maps(cfg, inputs):
    f = lambda a: np.ascontiguousarray(np.asarray(a, dtype=np.float32))
    shared = {}
    for k in _WEIGHT_KEYS:
        a = f(inputs[k])
        shared[k] = a.reshape(a.shape[1:])
    shared["rwkv_r_k"] = shared["rwkv_r_k"].reshape(-1)
    shared["hgrn_lb_logits"] = f(inputs["hgrn_lb_logits"])
    shared["hgrn_norm"] = f(inputs["hgrn_norm"]).reshape(-1)
    shared["norm_final"] = f(inputs["norm_final"])
    xp = f(inputs["x_prompt"])
    xs = f(inputs["x_sample"])
    c = f(inputs["c"])
    cctx = f(inputs["c_ctx"])
    sr = f(inputs["state_rwkv"])
    sh = f(inputs["state_hgrn"])
    maps = []
    for i in range(8):
        m = dict(shared)
        m["x_tok"] = np.ascontiguousarray(np.concatenate([xp[2 * i].reshape(-1, cfg.D), xp[2 * i + 1].reshape(-1, cfg.D),
                                                          xs[i].reshape(-1, cfg.D)], axis=0))
        m["cond"] = np.ascontiguousarray(np.stack([cctx, c[i]], axis=0))
        m["state_rwkv"] = np.ascontiguousarray(sr[i, 0])
        m["state_hgrn"] = np.ascontiguousarray(sh[i, 0])
        maps.append(m)
    return maps


def gather(cfg, res):
    D = cfg.D
    yp = np.zeros((16, 256, D), np.float32)
    ys = np.zeros((8, 4096, D), np.float32)
    nsr = np.zeros((16, 1, 2, cfg.HR, 64, 64), np.float32)
    nsh = np.zeros((16, 1, 2, cfg.HG, 128, 128), np.float32)
    for i in range(8):
        r = res[i]
        y = np.asarray(r["y_tok"])
        yp[2 * i] = y[0:256]
        yp[2 * i + 1] = y[256:512]
        ys[i] = y[512:]
        nsr[2 * i:2 * i + 2, 0] = np.asarray(r["ns_rwkv"])
        nsh[2 * i:2 * i + 2, 0] = np.asarray(r["ns_hgrn"])
    return yp, ys, nsr, nsh


def run(cfg, inputs, trace=False):
    nc = build(cfg)
    maps = make_in_maps(cfg, inputs)
    res = run_bass_kernel_spmd(nc, maps, core_ids=list(range(8)), trace=trace)
    return gather(cfg, res.results), res


def kernel(**inputs):
    cfg = Cfg()
    out, _ = run(cfg, inputs)
    return out
```
